# Optimizing a Trainium2 kernel written in Bass

```python
import jax, jax.numpy as jnp
from jax import lax
import numpy as np

D_MODEL = 2048
BATCH = 4
SEQ = 4096
DEPTH = 1

MLA_HEADS = 16
Q_LORA_RANK = 512
KV_LORA_RANK = 512
QK_NOPE_DIM = 128
QK_ROPE_DIM = 64
V_HEAD_DIM = 128
ROPE_THETA = 10000.0
Q_BLOCK = 128
SSD_EXPAND = 2
SSD_D_INNER = SSD_EXPAND * D_MODEL
SSD_HEAD_DIM = 64
SSD_HEADS = SSD_D_INNER // SSD_HEAD_DIM
SSD_GROUPS = 8
SSD_STATE = 128
SSD_CONV = 5
SSD_CHUNK = 128
SSD_CONV_DIM = SSD_D_INNER + 2 * SSD_GROUPS * SSD_STATE
FFN_DIM = 5632
FFN_CONV = 3
EPS = 1e-6

IN_WIDTHS = (Q_LORA_RANK, KV_LORA_RANK, QK_ROPE_DIM,
             SSD_D_INNER, SSD_CONV_DIM, SSD_HEADS, SSD_HEADS,
             D_MODEL, D_MODEL)
IN_DIM = sum(IN_WIDTHS)
IN_SPLITS = [int(v) for v in np.cumsum(IN_WIDTHS)[:-1]]

kernel_name = "bidir_hybrid_mla_ssd_convffn"


def rms_norm(x, w):
    xf = x.astype(jnp.float32)
    y = xf * lax.rsqrt(jnp.mean(xf * xf, axis=-1, keepdims=True) + EPS)
    return (y * w.astype(jnp.float32)).astype(x.dtype)


def centred_dwconv(x, w, b):
    k, c = w.shape
    y = lax.conv_general_dilated(
        x, w[:, None, :].astype(x.dtype), window_strides=(1,),
        padding=[((k - 1) // 2, k // 2)],
        dimension_numbers=("NWC", "WIO", "NWC"), feature_group_count=c)
    return y + b.astype(x.dtype)


def rope_tables(positions):
    half = QK_ROPE_DIM // 2
    inv_freq = ROPE_THETA ** (-jnp.arange(half, dtype=jnp.float32) / half)
    ang = positions.astype(jnp.float32)[..., None] * inv_freq
    return jnp.cos(ang), jnp.sin(ang)


def apply_rope(x, cos, sin):
    x1, x2 = jnp.split(x.astype(jnp.float32), 2, axis=-1)
    return jnp.concatenate([x1 * cos - x2 * sin, x1 * sin + x2 * cos], axis=-1).astype(x.dtype)


def mla_branch(q_lat, kv_lat, k_rope, cos, sin, q_norm_w, w_uq, kv_norm_w, w_ukv, w_o_attn):
    bsz, s_len, _ = q_lat.shape
    q = (rms_norm(q_lat, q_norm_w) @ w_uq).reshape(bsz, s_len, MLA_HEADS, QK_NOPE_DIM + QK_ROPE_DIM)
    q_nope, q_rope = q[..., :QK_NOPE_DIM], q[..., QK_NOPE_DIM:]
    q_rope = apply_rope(q_rope, cos[:, :, None, :], sin[:, :, None, :])
    kv = (rms_norm(kv_lat, kv_norm_w) @ w_ukv).reshape(bsz, s_len, MLA_HEADS, QK_NOPE_DIM + V_HEAD_DIM)
    k_nope, v = kv[..., :QK_NOPE_DIM], kv[..., QK_NOPE_DIM:]
    k_rope = apply_rope(k_rope, cos, sin)
    scale = (QK_NOPE_DIM + QK_ROPE_DIM) ** -0.5
    nb = s_len // Q_BLOCK
    qn_blocks = q_nope.reshape(bsz, nb, Q_BLOCK, MLA_HEADS, QK_NOPE_DIM).transpose(1, 0, 2, 3, 4)
    qr_blocks = q_rope.reshape(bsz, nb, Q_BLOCK, MLA_HEADS, QK_ROPE_DIM).transpose(1, 0, 2, 3, 4)

    def attend(blk):
        qn, qr = blk
        sc = (jnp.einsum("bqhd,bkhd->bhqk", qn, k_nope)
              + jnp.einsum("bqhr,bkr->bhqk", qr, k_rope))
        p = jax.nn.softmax(sc.astype(jnp.float32) * scale, axis=-1).astype(v.dtype)
        return jnp.einsum("bhqk,bkhd->bqhd", p, v)

    o = lax.map(attend, (qn_blocks, qr_blocks))
    o = o.transpose(1, 0, 2, 3, 4).reshape(bsz, s_len, MLA_HEADS * V_HEAD_DIM)
    return o @ w_o_attn


def ssd_scan(x, dt_raw, a_log, dt_bias, b_mat, c_mat):
    bsz, s_len = x.shape[:2]
    nc, q_len = s_len // SSD_CHUNK, SSD_CHUNK
    g, r = SSD_GROUPS, SSD_HEADS // SSD_GROUPS
    p, n = SSD_HEAD_DIM, SSD_STATE
    dt = jax.nn.softplus(dt_raw.astype(jnp.float32) + dt_bias.astype(jnp.float32))
    a = (-jnp.exp(a_log.astype(jnp.float32)) * dt).reshape(bsz, nc, q_len, g, r)
    xd = (x.astype(jnp.float32) * dt[..., None]).reshape(bsz, nc, q_len, g, r, p)
    bm = b_mat.astype(jnp.float32).reshape(bsz, nc, q_len, g, n)
    cm = c_mat.astype(jnp.float32).reshape(bsz, nc, q_len, g, n)
    a_cs = jnp.cumsum(a, axis=2)
    mask = jnp.tril(jnp.ones((q_len, q_len), dtype=bool))[:, :, None, None]
    seg = a_cs[:, :, :, None] - a_cs[:, :, None]
    decay = jnp.exp(jnp.where(mask, seg, -jnp.inf))
    cb = jnp.einsum("bclgn,bcsgn->bclsg", cm, bm)
    y_diag = jnp.einsum("bclsgr,bcsgrp->bclgrp", cb[..., None] * decay, xd)
    decay_to_end = jnp.exp(a_cs[:, :, -1:] - a_cs)
    states = jnp.einsum("bclgn,bclgrp->bcgrpn", bm, xd * decay_to_end[..., None])
    chunk_decay = jnp.exp(a_cs[:, :, -1])

    def step(h, inp):
        st, dec = inp
        return h * dec[..., None, None] + st, h

    h0 = jnp.zeros((bsz, g, r, p, n), jnp.float32)
    _, prev = lax.scan(step, h0, (states.transpose(1, 0, 2, 3, 4, 5), chunk_decay.transpose(1, 0, 2, 3)))
    prev = prev.transpose(1, 0, 2, 3, 4, 5)
    y_off = jnp.einsum("bclgn,bcgrpn->bclgrp", cm, prev) * jnp.exp(a_cs)[..., None]
    return (y_diag + y_off).reshape(bsz, s_len, SSD_HEADS, p)


def ssd_branch(z, xbc, dt_f, dt_b, conv_w, conv_b, a_log_f, a_log_b, dt_bias_f, dt_bias_b,
               d_skip, norm_w, w_o_ssd):
    bsz, s_len, _ = xbc.shape
    xbc = jax.nn.silu(centred_dwconv(xbc, conv_w, conv_b))
    xs, bm, cm = jnp.split(xbc, [SSD_D_INNER, SSD_D_INNER + SSD_GROUPS * SSD_STATE], axis=-1)
    xs = xs.reshape(bsz, s_len, SSD_HEADS, SSD_HEAD_DIM)
    bm = bm.reshape(bsz, s_len, SSD_GROUPS, SSD_STATE)
    cm = cm.reshape(bsz, s_len, SSD_GROUPS, SSD_STATE)
    flip = lambda t: jnp.flip(t, axis=1)
    y_f = ssd_scan(xs, dt_f, a_log_f, dt_bias_f, bm, cm)
    y_b = flip(ssd_scan(flip(xs), flip(dt_b), a_log_b, dt_bias_b, flip(bm), flip(cm)))
    y = (y_f + y_b + d_skip.astype(jnp.float32)[:, None] * xs.astype(jnp.float32)).astype(xs.dtype)
    y = y.reshape(bsz, s_len, SSD_D_INNER)
    y = rms_norm(y * jax.nn.silu(z), norm_w)
    return y @ w_o_ssd


def conv_ffn(h, w_up, conv_w, conv_b, w_down):
    u = centred_dwconv(h @ w_up, conv_w, conv_b)
    gate, val = jnp.split(u, 2, axis=-1)
    return (jax.nn.silu(gate) * val) @ w_down


def setup_inputs(seed: int = 0) -> dict:
    key = jax.random.key(seed)
    ks = jax.random.split(key, 32)
    L, D = DEPTH, D_MODEL

    def nrm(k, shape, scale):
        return jax.random.normal(k, shape, jnp.float32) * scale

    def gain(k, shape):
        return 1.0 + 0.02 * jax.random.normal(k, shape, jnp.float32)

    def dt_bias(k):
        dt = jnp.exp(jax.random.uniform(k, (L, SSD_HEADS), jnp.float32, np.log(1e-3), np.log(1e-1)))
        return dt + jnp.log(-jnp.expm1(-dt))

    def a_log(k):
        return jnp.log(jax.random.uniform(k, (L, SSD_HEADS), jnp.float32, 1.0, 16.0))

    offsets = jax.random.randint(ks[1], (BATCH, 1), 0, 1024, dtype=jnp.int32)
    positions = jnp.arange(SEQ, dtype=jnp.int32)[None, :] + offsets
    return {
        "x": nrm(ks[0], (BATCH, SEQ, D), 1.0),
        "positions": positions,
        "norm_mix_w": gain(ks[2], (L, D)),
        "w_in": nrm(ks[3], (L, D, IN_DIM), D ** -0.5),
        "q_norm_w": gain(ks[4], (L, Q_LORA_RANK)),
        "w_uq": nrm(ks[5], (L, Q_LORA_RANK, MLA_HEADS * (QK_NOPE_DIM + QK_ROPE_DIM)), Q_LORA_RANK ** -0.5),
        "kv_norm_w": gain(ks[6], (L, KV_LORA_RANK)),
        "w_ukv": nrm(ks[7], (L, KV_LORA_RANK, MLA_HEADS * (QK_NOPE_DIM + V_HEAD_DIM)), KV_LORA_RANK ** -0.5),
        "w_o_attn": nrm(ks[8], (L, MLA_HEADS * V_HEAD_DIM, D), (MLA_HEADS * V_HEAD_DIM) ** -0.5),
        "ssd_conv_w": nrm(ks[9], (L, SSD_CONV, SSD_CONV_DIM), SSD_CONV ** -0.5),
        "ssd_conv_b": nrm(ks[10], (L, SSD_CONV_DIM), 0.02),
        "a_log_fwd": a_log(ks[11]),
        "a_log_bwd": a_log(ks[12]),
        "dt_bias_fwd": dt_bias(ks[13]),
        "dt_bias_bwd": dt_bias(ks[14]),
        "ssd_d": gain(ks[15], (L, SSD_HEADS)),
        "ssd_norm_w": gain(ks[16], (L, SSD_D_INNER)),
        "w_o_ssd": nrm(ks[17], (L, SSD_D_INNER, D), SSD_D_INNER ** -0.5),
        "w_out": nrm(ks[18], (L, D, D), D ** -0.5),
        "norm_ffn_w": gain(ks[19], (L, D)),
        "ffn_w_up": nrm(ks[20], (L, D, 2 * FFN_DIM), D ** -0.5),
        "ffn_conv_w": nrm(ks[21], (L, FFN_CONV, 2 * FFN_DIM), FFN_CONV ** -0.5),
        "ffn_conv_b": nrm(ks[22], (L, 2 * FFN_DIM), 0.02),
        "ffn_w_down": nrm(ks[23], (L, FFN_DIM, D), FFN_DIM ** -0.5),
        "norm_final_w": gain(ks[24], (D,)),
    }


def reference(x, positions, norm_mix_w, w_in, q_norm_w, w_uq, kv_norm_w, w_ukv, w_o_attn,
              ssd_conv_w, ssd_conv_b, a_log_fwd, a_log_bwd, dt_bias_fwd, dt_bias_bwd, ssd_d,
              ssd_norm_w, w_o_ssd, w_out, norm_ffn_w, ffn_w_up, ffn_conv_w, ffn_conv_b,
              ffn_w_down, norm_final_w):
    cos, sin = rope_tables(positions)
    h = x
    for l in range(DEPTH):
        n = rms_norm(h, norm_mix_w[l])
        u = n @ w_in[l]
        q_lat, kv_lat, k_rope, z, xbc, dt_f, dt_b, g_attn, g_ssd = jnp.split(u, IN_SPLITS, axis=-1)
        attn_out = mla_branch(q_lat, kv_lat, k_rope, cos, sin, q_norm_w[l], w_uq[l],
                              kv_norm_w[l], w_ukv[l], w_o_attn[l])
        ssd_out = ssd_branch(z, xbc, dt_f, dt_b, ssd_conv_w[l], ssd_conv_b[l], a_log_fwd[l],
                             a_log_bwd[l], dt_bias_fwd[l], dt_bias_bwd[l], ssd_d[l],
                             ssd_norm_w[l], w_o_ssd[l])
        mixed = jax.nn.sigmoid(g_attn) * attn_out + jax.nn.sigmoid(g_ssd) * ssd_out
        h = h + mixed @ w_out[l]
        h = h + conv_ffn(rms_norm(h, norm_ffn_w[l]), ffn_w_up[l], ffn_conv_w[l], ffn_conv_b[l],
                         ffn_w_down[l])
    return rms_norm(h, norm_final_w)
```

```python
import numpy as np
from contextlib import ExitStack
import concourse.bass as bass
import concourse.mybir as mybir
from concourse.bass_utils import run_bass_kernel_spmd

F32, BF16, I32 = mybir.dt.float32, mybir.dt.bfloat16, mybir.dt.int32
AF = mybir.ActivationFunctionType
ALU = mybir.AluOpType

D = 2048
KC = 16
S = 4096
NOWN = 2176
NOUT = 2048
NOTH = S - NOWN
NCH_OWN = 17
EPS = 1e-6
HEADS = 16
SSD_H = 64
DI = 4096
FFN = 5632
SCALE = 192 ** -0.5

OWN_BLOCKS = [(0, 512), (512, 512), (1024, 512), (1536, 512), (2048, 128)]
OTH_BLOCKS = [(0, 512), (512, 512), (1024, 512), (1536, 384)]
ALL_BLOCKS = [(i * 512, 512) for i in range(8)]


class Sem:
    def __init__(self, h):
        self.h = h
        self.val = 0


class Eng:
    def __init__(self, name, e, sem):
        self.name = name
        self.e = e
        self.sem = sem
        self.seen = {}


class T:
    def __init__(self, h, name):
        self.h = h
        self.name = name
        self.w = None
        self.rs = {}
        self.dsem = None

    def __getitem__(self, k):
        return self.h[k]


class K:
    def __init__(self, nc):
        self.nc = nc
        self.all_sems = []
        self.pe = self._eng("pe", nc.tensor)
        self.act = self._eng("act", nc.scalar)
        self.dve = self._eng("dve", nc.vector)
        self.pool = self._eng("pool", nc.gpsimd)
        self.sp = self._eng("sp", nc.sync)
        self.engs = [self.pe, self.act, self.dve, self.pool, self.sp]
        self.free_dsems = []
        self.n_dsem = 0
        self.phase_tiles = []
        self.uid = 0
        self.rr = 0

    def _eng(self, name, e):
        s = Sem(self.nc.alloc_semaphore("es_" + name))
        self.all_sems.append(s)
        return Eng(name, e, s)

    def take_dsem(self):
        if self.free_dsems:
            return self.free_dsems.pop()
        s = Sem(self.nc.alloc_semaphore("ds%d" % self.n_dsem))
        self.n_dsem += 1
        self.all_sems.append(s)
        return s

    def sb(self, st, shape, dt, name=None):
        self.uid += 1
        nm = "%s_%d" % (name or "sb", self.uid)
        h = st.enter_context(self.nc.sbuf_tensor(nm, list(shape), dt))
        t = T(h, nm)
        self.phase_tiles.append(t)
        return t

    def ps(self, st, shape, dt=F32, name=None):
        self.uid += 1
        nm = "%s_%d" % (name or "ps", self.uid)
        h = st.enter_context(self.nc.psum_tensor(nm, list(shape), dt))
        t = T(h, nm)
        self.phase_tiles.append(t)
        return t

    def _wait(self, E, deps):
        best = {}
        for (s, v) in deps:
            if best.get(s, (None, 0))[1] < v:
                best[s] = (s, v)
        for (s, v) in best.values():
            if E.seen.get(s, 0) < v:
                E.e.wait_ge(s.h, v)
                E.seen[s] = v

    def op(self, E, reads, writes, fn):
        deps = []
        for t in reads:
            if t.w is not None:
                deps.append(t.w)
        for t in writes:
            if t.w is not None and t.w[0] is not E.sem:
                deps.append(t.w)
            for s, v in t.rs.items():
                if s is not E.sem:
                    deps.append((s, v))
        self._wait(E, deps)
        ins = fn()
        E.sem.val += 1
        ins.then_inc(E.sem.h, 1)
        tok = (E.sem, E.sem.val)
        for t in reads:
            t.rs[E.sem] = E.sem.val
        for t in writes:
            t.w = tok
            t.rs = {}
        return ins

    def dma(self, Q, out, in_, reads, writes, st, **kw):
        deps = []
        for t in reads:
            if t.w is not None:
                deps.append(t.w)
        for t in writes:
            if t.w is not None:
                deps.append(t.w)
            for s, v in t.rs.items():
                deps.append((s, v))
        self._wait(Q, deps)
        if st.dsem is None:
            st.dsem = self.take_dsem()
        ins = Q.e.dma_start(out=out, in_=in_, **kw)
        st.dsem.val += 16
        ins.then_inc(st.dsem.h, 16)
        tok = (st.dsem, st.dsem.val)
        for t in reads:
            t.rs[st.dsem] = st.dsem.val
        for t in writes:
            t.w = tok
            t.rs = {}

    def load(self, tile, dst_ap, src_ap, **kw):
        self.dma(self.sp, dst_ap, src_ap, [], [tile], tile, **kw)

    def store(self, tile, dst_ap, src_ap, **kw):
        self.dma(self.sp, dst_ap, src_ap, [tile], [], tile, **kw)

    def barrier(self, skip=()):
        for E in self.engs:
            for s in self.all_sems:
                if s in skip:
                    continue
                if s.val > 0 and E.seen.get(s, 0) < s.val:
                    E.e.wait_ge(s.h, s.val)
                    E.seen[s] = s.val
        for t in self.phase_tiles:
            if t.dsem is not None:
                self.free_dsems.append(t.dsem)
                t.dsem = None
        self.phase_tiles = []

    def evac_eng(self):
        self.rr += 1
        return self.act if (self.rr & 1) else self.dve

    def mm(self, ps_t, out_ap, lhsT_t, lhsT_ap, rhs_t, rhs_ap, start, stop):
        nc = self.nc
        return self.op(self.pe, [lhsT_t, rhs_t], [ps_t],
                       lambda: nc.tensor.matmul(out_ap, lhsT=lhsT_ap, rhs=rhs_ap, start=start, stop=stop))

    def copy(self, E, out_t, out_ap, in_t, in_ap):
        nc = self.nc
        if E is self.act:
            return self.op(E, [in_t], [out_t], lambda: nc.scalar.activation(out=out_ap, in_=in_ap, func=AF.Copy))
        elif E is self.dve:
            return self.op(E, [in_t], [out_t], lambda: nc.vector.tensor_copy(out=out_ap, in_=in_ap))
        else:
            return self.op(E, [in_t], [out_t], lambda: nc.gpsimd.tensor_copy(out=out_ap, in_=in_ap))

    def actf(self, out_t, out_ap, in_t, in_ap, func, extra_reads=(), **kw):
        nc = self.nc
        return self.op(self.act, [in_t] + list(extra_reads), [out_t],
                       lambda: nc.scalar.activation(out=out_ap, in_=in_ap, func=func, **kw))


def cdiv(a, b):
    return (a + b - 1) // b


class Builder:
    def __init__(self, dbg=False, upto=99):
        self.dbg = dbg
        self.upto = upto
        self.nc = bass.Bass("TRN2", target_bir_lowering=False)
        self.k = K(self.nc)
        self.inp = {}
        self.scr = {}
        self.spread_casts = False

    def din(self, name, shape, dt=F32):
        self.inp[name] = self.nc.dram_tensor(name, list(shape), dt, kind="ExternalInput").ap()
        return self.inp[name]

    def dscr(self, name, shape, dt, out=False):
        kind = "ExternalOutput" if (out or (self.dbg and name in self.dbg)) else "Internal"
        self.scr[name] = self.nc.dram_tensor(name, list(shape), dt, kind=kind).ap()
        return self.scr[name]

    def declare(self):
        din, ds = self.din, self.dscr
        din("xl", [S, D]); din("pos", [1, S], I32)
        din("w_in", [D, 15552]); din("w_dt", [D, 128])
        din("w_uqn", [512, 2048]); din("w_uqr", [512, 1024]); din("w_uqrot", [512, 1024])
        din("w_k", [512, 2048]); din("w_v", [512, 2048])
        din("w_oattn", [2048, 2048]); din("w_ossd", [4096, 2048]); din("w_out", [2048, 2048])
        din("w_up", [2048, 2 * FFN]); din("w_down", [FFN, 2048])
        din("nw_mix", [1, D]); din("nw_ffn", [1, D]); din("nw_fin", [1, D]); din("nw_ssd", [1, DI])
        din("nw_q", [128, 4]); din("nw_kv", [128, 4])
        din("cw", [128, 48 * 5]); din("cb", [128, 48]); din("cb_row", [1, 6144])
        din("fw", [128, 88 * 3]); din("fb", [128, 88])
        din("alog", [1, 128]); din("dtb", [1, 128]); din("dsk", [1, 64])
        din("c_ident", [128, 128]); din("c_tri", [128, 256]); din("c_negm", [128, 256])
        din("c_invf", [128, 1]); din("c_sgn", [128, 1])
        for nm, shp in [("w_ossd", [4096, 2048]), ("w_down", [FFN, 2048])]:
            ds("b_" + nm, shp, BF16)
        ds("nT", [D, S], BF16)
        ds("qlatT", [512, NOWN], F32)
        ds("kvT", [576, S], F32)
        ds("xbcT", [6144, S], BF16)
        ds("gT", [4096, NOWN], BF16)
        ds("sz", [NOWN, DI], BF16)
        ds("dtraw", [S, 128], F32)
        ds("qnT", [HEADS * 128, NOWN], BF16)
        ds("qrT", [HEADS * 64, NOWN], BF16)
        ds("knT", [HEADS * 128, S], BF16)
        ds("krT", [64, S], BF16)
        ds("Vs", [HEADS * 128, 32 * 128], BF16)
        ds("oT", [2048, NOWN], BF16)
        ds("gaoT", [2048, NOWN], F32)
        ds("xs_tok", [S, DI], BF16)
        ds("B_tok", [S, 1024], BF16)
        ds("BCT", [2048, NOWN], BF16)
        ds("yf", [NOWN, DI], F32)
        ds("yn", [NOWN, DI], BF16)
        ds("mixT", [2048, NOWN], BF16)
        ds("h1", [NOWN, D], F32)
        ds("n2T", [D, NOWN], BF16)
        ds("aT", [FFN, NOUT], BF16)
        self.eps_t = T(self.nc.alloc_sbuf_tensor("eps_c", [128, 1], F32), "eps_c")
        self.k.op(self.k.dve, [], [self.eps_t], lambda: self.nc.vector.memset(self.eps_t[:], EPS))
        self.yout = self.nc.dram_tensor("y", [NOUT, D], F32, kind="ExternalOutput").ap()

    def phase_cast(self):
        k, nc = self.k, self.nc
        self.cast_t = T(None, "cast")
        self.cast_t.dsem = k.take_dsem()
        self.pending_casts = []
        for nm in ["w_ossd", "w_down"]:
            src = self.inp[nm]
            dst = self.scr["b_" + nm]
            R, C = src.shape
            per_row = cdiv(C * 4, 8192)
            rows = max(1, min(R, 1024 // per_row))
            for r0 in range(0, R, rows):
                r1 = min(R, r0 + rows)
                self.pending_casts.append((dst[r0:r1, :], src[r0:r1, :]))

    def issue_cast(self, n=1):
        nc = self.nc
        for _ in range(n):
            if not self.pending_casts:
                return
            dst, src = self.pending_casts.pop(0)
            ins = nc.gpsimd.dma_start(out=dst, in_=src, max_dma_last_dim=8192)
            self.cast_t.dsem.val += 16
            ins.then_inc(self.cast_t.dsem.h, 16)

    def rstd_from_ss(self, rstd_t, ss_t, n, np_=128):
        k, nc = self.k, self.nc
        k.op(k.act, [ss_t], [rstd_t], lambda: nc.scalar.activation(
            out=rstd_t[:], in_=ss_t[:], func=AF.Sqrt, scale=1.0 / n, bias=self.eps_t[:np_, 0:1]))
        k.op(k.dve, [rstd_t], [rstd_t], lambda: nc.vector.reciprocal(out=rstd_t[:], in_=rstd_t[:]))

    def norm_transpose(self, st, src_rows_fn, ntiles, nw_name, dstT, tok_off=0, extra=None):
        raise NotImplementedError

    def phase_A(self):
        k, nc = self.k, self.nc
        with ExitStack() as st:
            ident = k.sb(st, [128, 128], F32, "ident")
            identb = k.sb(st, [128, 128], BF16, "identb")
            wb = k.sb(st, [128, D], F32, "nw")
            xin = [k.sb(st, [128, D], F32, "xin") for _ in range(2)]
            xn = [k.sb(st, [128, D], BF16, "xn") for _ in range(2)]
            junk = k.sb(st, [128, D], BF16, "junk")
            ss = [k.sb(st, [128, 1], F32, "ss") for _ in range(2)]
            rstd = [k.sb(st, [128, 1], F32, "rstd") for _ in range(2)]
            stage = [k.sb(st, [128, KC, 512], BF16, "stage") for _ in range(2)]
            pt = [k.ps(st, [128, 8, 128], BF16, "pt") for _ in range(4)]
            k.load(ident, ident[:], self.inp["c_ident"])
            k.load(wb, wb[:], self.inp["nw_mix"].partition_broadcast(128))
            k.copy(k.dve, identb, identb[:], ident, ident[:])
            nT = self.scr["nT"].rearrange("(kc p) t -> p kc t", p=128)
            for tt in range(32):
                g, i = tt // 4, tt % 4
                xi, xo = xin[tt % 2], xn[tt % 2]
                k.load(xi, xi[:], self.inp["xl"][tt * 128:(tt + 1) * 128, :])
                s_, r_ = ss[tt % 2], rstd[tt % 2]
                k.op(k.act, [xi], [junk, s_], lambda: nc.scalar.activation(
                    out=junk[:], in_=xi[:], func=AF.Square, accum_out=s_[:]))
                self.rstd_from_ss(r_, s_, D)
                k.op(k.dve, [xi, r_, wb], [xo], lambda: nc.vector.scalar_tensor_tensor(
                    out=xo[:], in0=xi[:], scalar=r_[:, 0:1], in1=wb[:], op0=ALU.mult, op1=ALU.mult))
                for half in range(2):
                    p_ = pt[(2 * tt + half) % 4]
                    for j in range(8):
                        kc = half * 8 + j
                        k.op(k.pe, [xo, identb], [p_], lambda: nc.tensor.transpose(
                            out=p_[:, j, :], in_=xo[:, kc * 128:(kc + 1) * 128], identity=identb[:]))
                    sg = stage[g % 2]
                    k.copy(k.evac_eng(), sg, sg[:, half * 8:(half + 1) * 8, i * 128:(i + 1) * 128], p_, p_[:])
                if i == 3:
                    sg = stage[g % 2]
                    k.store(sg, nT[:, :, g * 512:(g + 1) * 512], sg[:])
            k.barrier(skip=(self.cast_t.dsem,))

    def wload(self, w_, dst, src, cast):
        k = self.k
        if cast:
            k.dma(k.pool, dst, src, [], [w_], w_)
            if self.spread_casts:
                self.issue_cast(1)
        else:
            k.load(w_, dst, src)

    def proj_fm(self, st, wsrc, F, xT, kc_n, blocks, evac, wt, psums, wcols=512, cast=False):
        k, nc = self.k, self.nc
        wv = wsrc.rearrange("(kc p) f -> p kc f", p=128)
        cnt = 0
        nW = cdiv(F, wcols)

        def ld(wi):
            cols = min(wcols, F - wi * wcols)
            w_ = wt[wi % len(wt)]
            self.wload(w_, w_[:, :, :cols], wv[:, :, wi * wcols:wi * wcols + cols], cast)
        ld(0)
        for wi in range(nW):
            if wi + 1 < nW:
                ld(wi + 1)
            cols = min(wcols, F - wi * wcols)
            w_ = wt[wi % len(wt)]
            for fc in range(cdiv(cols, 128)):
                fw = min(128, cols - fc * 128)
                for bi, (t0, tw) in enumerate(blocks):
                    p_ = psums[cnt % len(psums)]
                    cnt += 1
                    for kc in range(kc_n):
                        k.mm(p_, p_[:fw, :tw], w_, w_[:, kc, fc * 128:fc * 128 + fw], xT, xT[:, kc, t0:t0 + tw],
                             kc == 0, kc == kc_n - 1)
                    evac(p_, wi * wcols + fc * 128, fw, t0, tw, bi == len(blocks) - 1)

    def proj_tm(self, st, wsrc, F, xT, kc_n, tiles, evac, wt, psums, wcols=512, cast=False):
        k, nc = self.k, self.nc
        wv = wsrc.rearrange("(kc p) f -> p kc f", p=128)
        cnt = 0
        nW = cdiv(F, wcols)

        def ld(wi):
            cols = min(wcols, F - wi * wcols)
            w_ = wt[wi % len(wt)]
            self.wload(w_, w_[:, :, :cols], wv[:, :, wi * wcols:wi * wcols + cols], cast)
        ld(0)
        for wi in range(nW):
            if wi + 1 < nW:
                ld(wi + 1)
            cols = min(wcols, F - wi * wcols)
            w_ = wt[wi % len(wt)]
            for tt in tiles:
                p_ = psums[cnt % len(psums)]
                cnt += 1
                for kc in range(kc_n):
                    k.mm(p_, p_[:, :cols], xT, xT[:, kc, tt * 128:(tt + 1) * 128], w_, w_[:, kc, :cols],
                         kc == 0, kc == kc_n - 1)
                evac(p_, tt, wi * wcols, cols)

    def row_evac(self, st, dst, ntok, dt, func=None, tok_off=0, nbuf=3, engs=None):
        k, nc = self.k, self.nc
        stg = [k.sb(st, [128, ntok], dt, "rowstg") for _ in range(nbuf)]
        state = {"i": 0}

        def evac(p_, f0, fw, t0, tw, last):
            s_ = stg[state["i"] % nbuf]
            if func is None:
                k.copy(k.evac_eng(), s_, s_[:fw, t0:t0 + tw], p_, p_[:fw, :tw])
            else:
                k.actf(s_, s_[:fw, t0:t0 + tw], p_, p_[:fw, :tw], func)
            if last:
                k.store(s_, dst[f0:f0 + fw, tok_off:tok_off + ntok], s_[:fw, :])
                state["i"] += 1
        return evac

    def phase_B(self, own):
        k, nc = self.k, self.nc
        self.spread_casts = True
        ntok = NOWN if own else NOTH
        tok0 = 0 if own else NOWN
        blocks = OWN_BLOCKS if own else OTH_BLOCKS
        with ExitStack() as st:
            xT = k.sb(st, [128, KC, ntok], BF16, "xT")
            wt = [k.sb(st, [128, KC, 512], BF16, "wt") for _ in range(2)]
            psums = [k.ps(st, [128, 512], F32, "pB") for _ in range(6)]
            nT = self.scr["nT"].rearrange("(kc p) t -> p kc t", p=128)
            for kc in range(KC):
                k.load(xT, xT[:, kc, :], nT[:, kc, tok0:tok0 + ntok])
            bw = self.inp["w_in"]
            ev32 = self.row_evac(st, self.scr["kvT"], ntok, F32, tok_off=tok0, nbuf=2)
            self.proj_fm(st, bw[:, 512:1088], 576, xT, KC, blocks, ev32, wt, psums, cast=True)
            evb = self.row_evac(st, self.scr["xbcT"], ntok, BF16, tok_off=tok0, nbuf=3)
            self.proj_fm(st, bw[:, 5184:11328], 6144, xT, KC, blocks, evb, wt, psums, cast=True)
            if own:
                evq = self.row_evac(st, self.scr["qlatT"], ntok, F32, nbuf=2)
                self.proj_fm(st, bw[:, 0:512], 512, xT, KC, blocks, evq, wt, psums, cast=True)
                evg = self.row_evac(st, self.scr["gT"], ntok, BF16, func=AF.Sigmoid, nbuf=3)
                self.proj_fm(st, bw[:, 11456:15552], 4096, xT, KC, blocks, evg, wt, psums, cast=True)
            stg_dt = [k.sb(st, [128, 128], F32, "stgdt") for _ in range(3)]
            cnt = {"i": 0}
            dtraw = self.scr["dtraw"]

            def ev_dt(p_, tt, c0, cols):
                s_ = stg_dt[cnt["i"] % 3]
                cnt["i"] += 1
                k.copy(k.evac_eng(), s_, s_[:, :], p_, p_[:, :128])
                k.store(s_, dtraw[tok0 + tt * 128:tok0 + (tt + 1) * 128, :], s_[:, :])
            self.proj_tm(st, self.inp["w_dt"], 128, xT, KC, list(range(ntok // 128)), ev_dt, wt, psums, cast=True)
            if own:
                stg_z = [k.sb(st, [128, 512], BF16, "stgz") for _ in range(3)]
                sz = self.scr["sz"]

                def ev_z(p_, tt, c0, cols):
                    s_ = stg_z[cnt["i"] % 3]
                    cnt["i"] += 1
                    k.actf(s_, s_[:, :cols], p_, p_[:, :cols], AF.Silu)
                    k.store(s_, sz[tt * 128:(tt + 1) * 128, c0:c0 + cols], s_[:, :cols])
                self.proj_tm(st, bw[:, 1088:5184], 4096, xT, KC, list(range(NCH_OWN)), ev_z, wt, psums, cast=True)
            self.spread_casts = False
            if not own:
                self.issue_cast(10 ** 6)
            k.barrier()

    def rope_tables(self, st):
        import math
        k, nc = self.k, self.nc
        pi_ = k.sb(st, [128, S], I32, "posi")
        ang = k.sb(st, [128, S], F32, "ang")
        tmp = k.sb(st, [128, S], F32, "rtmp")
        cos2 = k.sb(st, [128, S], F32, "cos2")
        sin2 = k.sb(st, [128, S], F32, "sin2")
        invf = k.sb(st, [128, 1], F32, "invf")
        sgn = k.sb(st, [128, 1], F32, "sgn")
        k.load(pi_, pi_[:], self.inp["pos"].partition_broadcast(128))
        k.load(invf, invf[:], self.inp["c_invf"])
        k.load(sgn, sgn[:], self.inp["c_sgn"])
        k.op(k.dve, [pi_], [ang], lambda: nc.vector.tensor_copy(out=ang[:], in_=pi_[:]))
        k.op(k.dve, [ang, invf], [ang], lambda: nc.vector.tensor_scalar(
            out=ang[:], in0=ang[:], scalar1=invf[:, 0:1], scalar2=None, op0=ALU.mult))
        ki = pi_
        for which, dst in ((0, sin2), (1, cos2)):
            sh = 0.0 if which == 0 else 0.5 * math.pi
            k.op(k.dve, [ang], [tmp], lambda: nc.vector.tensor_scalar(
                out=tmp[:], in0=ang[:], scalar1=sh, scalar2=1.0 / (2 * math.pi), op0=ALU.add, op1=ALU.mult))
            k.op(k.dve, [tmp], [ki], lambda: nc.vector.tensor_copy(out=ki[:], in_=tmp[:]))
            k.op(k.dve, [ki], [tmp], lambda: nc.vector.tensor_copy(out=tmp[:], in_=ki[:]))
            k.op(k.dve, [tmp, ang], [tmp], lambda: nc.vector.scalar_tensor_tensor(
                out=tmp[:], in0=tmp[:], scalar=-2 * math.pi, in1=ang[:], op0=ALU.mult, op1=ALU.add))
            k.op(k.dve, [tmp], [tmp], lambda: nc.vector.tensor_scalar(
                out=tmp[:], in0=tmp[:], scalar1=sh, scalar2=None, op0=ALU.add))
            k.op(k.dve, [tmp], [tmp], lambda: nc.vector.tensor_scalar(
                out=tmp[:], in0=tmp[:], scalar1=-3.1415925, scalar2=3.1415925, op0=ALU.max, op1=ALU.min))
            k.actf(dst, dst[:], tmp, tmp[:], AF.Sin)
        k.op(k.dve, [sin2, sgn], [sin2], lambda: nc.vector.tensor_scalar(
            out=sin2[:], in0=sin2[:], scalar1=sgn[:, 0:1], scalar2=None, op0=ALU.mult))
        return cos2, sin2, ang, tmp

    def lat_norm_block(self, xsrc_t, xsrc_ap_fn, nw, onesf, sq, pss, rb, out_t, out_ap_fn, tw):
        k, nc = self.k, self.nc
        for kc in range(4):
            k.actf(sq, sq[:, kc, :tw], xsrc_t, xsrc_ap_fn(kc), AF.Square)
        for kc in range(4):
            k.mm(pss, pss[:, :tw], onesf, onesf[:], sq, sq[:, kc, :tw], kc == 0, kc == 3)
        k.op(k.act, [pss], [rb], lambda: nc.scalar.activation(
            out=rb[:, :tw], in_=pss[:, :tw], func=AF.Sqrt, scale=1.0 / 512, bias=self.eps_t[:, 0:1]))
        k.op(k.dve, [rb], [rb], lambda: nc.vector.reciprocal(out=rb[:, :tw], in_=rb[:, :tw]))
        for kc in range(4):
            k.op(k.dve, [xsrc_t, nw, rb], [out_t], lambda: nc.vector.scalar_tensor_tensor(
                out=out_ap_fn(kc), in0=xsrc_ap_fn(kc), scalar=nw[:, kc:kc + 1], in1=rb[:, :tw],
                op0=ALU.mult, op1=ALU.mult))

    def proj_fm_res(self, w_, F, xT, kc_n, blocks, evac, psums, f_start=0):
        k = self.k
        cnt = 0
        for fc in range(cdiv(F, 128)):
            fw = min(128, F - fc * 128)
            for bi, (t0, tw) in enumerate(blocks):
                p_ = psums[cnt % len(psums)]
                cnt += 1
                for kc in range(kc_n):
                    k.mm(p_, p_[:fw, :tw], w_, w_[:, kc, f_start + fc * 128:f_start + fc * 128 + fw],
                         xT, xT[:, kc, t0:t0 + tw], kc == 0, kc == kc_n - 1)
                evac(p_, fc * 128, fw, t0, tw, bi == len(blocks) - 1)

    def phase_C1(self):
        k, nc = self.k, self.nc
        with ExitStack() as st:
            cos2, sin2, ka, kb = self.rope_tables(st)
            onesf = k.sb(st, [128, 128], F32, "onesf")
            k.op(k.dve, [], [onesf], lambda: nc.vector.memset(onesf[:], 1.0))
            nwq = k.sb(st, [128, 4], F32, "nwq")
            k.load(nwq, nwq[:], self.inp["nw_q"])
            qlat = k.sb(st, [128, 4, NOWN], F32, "qlat")
            k.load(qlat, qlat[:], self.scr["qlatT"].rearrange("(kc p) t -> p kc t", p=128))
            qn = k.sb(st, [128, 4, NOWN], BF16, "qn")
            sq = k.sb(st, [128, 4, 512], F32, "sq")
            rb = k.sb(st, [128, 512], F32, "rb")
            psums = [k.ps(st, [128, 512], F32, "pC") for _ in range(6)]
            pss = k.ps(st, [128, 512], F32, "pss")
            wn = k.sb(st, [128, 4, 2048], BF16, "wn")
            wr = k.sb(st, [128, 4, 1024], BF16, "wr")
            wrot = k.sb(st, [128, 4, 1024], BF16, "wrot")
            self.wload(wn, wn[:], self.inp["w_uqn"].rearrange("(kc p) f -> p kc f", p=128), True)
            self.wload(wr, wr[:], self.inp["w_uqr"].rearrange("(kc p) f -> p kc f", p=128), True)
            self.wload(wrot, wrot[:], self.inp["w_uqrot"].rearrange("(kc p) f -> p kc f", p=128), True)
            for (t0, tw) in OWN_BLOCKS:
                self.lat_norm_block(qlat, lambda kc: qlat[:, kc, t0:t0 + tw], nwq, onesf, sq, pss, rb,
                                    qn, lambda kc: qn[:, kc, t0:t0 + tw], tw)
            ev = self.row_evac(st, self.scr["qnT"], NOWN, BF16, nbuf=2)
            self.proj_fm_res(wn, 2048, qn, 4, OWN_BLOCKS, ev, psums)
            t1 = [k.sb(st, [128, 512], F32, "t1") for _ in range(2)]
            t2 = [k.sb(st, [128, 512], F32, "t2") for _ in range(2)]
            rstg = [k.sb(st, [128, NOWN], BF16, "rstg") for _ in range(2)]
            cnt = 0
            for hp in range(8):
                sg = rstg[hp % 2]
                for bi, (t0, tw) in enumerate(OWN_BLOCKS):
                    pa = psums[cnt % 6]; pb = psums[(cnt + 1) % 6]; cnt += 2
                    for kc in range(4):
                        k.mm(pa, pa[:, :tw], wr, wr[:, kc, hp * 128:(hp + 1) * 128], qn, qn[:, kc, t0:t0 + tw], kc == 0, kc == 3)
                    for kc in range(4):
                        k.mm(pb, pb[:, :tw], wrot, wrot[:, kc, hp * 128:(hp + 1) * 128], qn, qn[:, kc, t0:t0 + tw], kc == 0, kc == 3)
                    a_, b_ = t1[bi % 2], t2[bi % 2]
                    k.op(k.dve, [pa, cos2], [a_], lambda: nc.vector.tensor_tensor(
                        out=a_[:, :tw], in0=pa[:, :tw], in1=cos2[:, t0:t0 + tw], op=ALU.mult))
                    k.op(k.dve, [pb, sin2], [b_], lambda: nc.vector.tensor_tensor(
                        out=b_[:, :tw], in0=pb[:, :tw], in1=sin2[:, t0:t0 + tw], op=ALU.mult))
                    k.op(k.pool, [a_, b_], [sg], lambda: nc.gpsimd.tensor_tensor(
                        out=sg[:, t0:t0 + tw], in0=a_[:, :tw], in1=b_[:, :tw], op=ALU.add))
                k.store(sg, self.scr["qrT"][hp * 128:(hp + 1) * 128, :], sg[:])
            kro = k.sb(st, [64, S], BF16, "kro")
            kvT = self.scr["kvT"]
            k.load(ka, ka[0:64, :], kvT[512:576, :])
            k.load(kb, kb[0:32, :], kvT[544:576, :])
            k.load(kb, kb[32:64, :], kvT[512:544, :])
            k.op(k.dve, [ka, cos2], [ka], lambda: nc.vector.tensor_tensor(out=ka[0:64, :], in0=ka[0:64, :], in1=cos2[0:64, :], op=ALU.mult))
            k.op(k.pool, [kb, sin2], [kb], lambda: nc.gpsimd.tensor_tensor(out=kb[0:64, :], in0=kb[0:64, :], in1=sin2[0:64, :], op=ALU.mult))
            k.op(k.dve, [ka, kb], [kro], lambda: nc.vector.tensor_tensor(out=kro[:, :], in0=ka[0:64, :], in1=kb[0:64, :], op=ALU.add))
            k.store(kro, self.scr["krT"][:, :], kro[:, :])
            k.barrier()

    def phase_C2(self):
        k, nc = self.k, self.nc
        with ExitStack() as st:
            onesf = k.sb(st, [128, 128], F32, "onesf")
            k.op(k.dve, [], [onesf], lambda: nc.vector.memset(onesf[:], 1.0))
            nwk = k.sb(st, [128, 4], F32, "nwk")
            k.load(nwk, nwk[:], self.inp["nw_kv"])
            kvn = k.sb(st, [128, 4, S], BF16, "kvn")
            xblk = [k.sb(st, [128, 4, 512], F32, "kvblk") for _ in range(2)]
            sq = k.sb(st, [128, 4, 512], F32, "sq")
            rb = k.sb(st, [128, 512], F32, "rb")
            psums = [k.ps(st, [128, 512], F32, "pC") for _ in range(6)]
            pss = k.ps(st, [128, 512], F32, "pss")
            wk = k.sb(st, [128, 4, 2048], BF16, "wk")
            wv = k.sb(st, [128, 4, 2048], BF16, "wv")
            self.wload(wk, wk[:], self.inp["w_k"].rearrange("(kc p) f -> p kc f", p=128), True)
            self.wload(wv, wv[:], self.inp["w_v"].rearrange("(kc p) f -> p kc f", p=128), True)
            kvT = self.scr["kvT"][0:512, :].rearrange("(kc p) t -> p kc t", p=128)
            for bi, (t0, tw) in enumerate(ALL_BLOCKS):
                xb = xblk[bi % 2]
                k.load(xb, xb[:], kvT[:, :, t0:t0 + tw])
                self.lat_norm_block(xb, lambda kc: xb[:, kc, :tw], nwk, onesf, sq, pss, rb,
                                    kvn, lambda kc: kvn[:, kc, t0:t0 + tw], tw)
            ev = self.row_evac(st, self.scr["knT"], S, BF16, nbuf=2)
            self.proj_fm_res(wk, 2048, kvn, 4, ALL_BLOCKS, ev, psums)
            vst = k.sb(st, [128, 4, 32, 128], BF16, "vst")
            Vs = self.scr["Vs"]
            cnt = 0
            for hg in range(4):
                for tt in range(32):
                    p_ = psums[cnt % 6]; cnt += 1
                    for kc in range(4):
                        k.mm(p_, p_[:, :], kvn, kvn[:, kc, tt * 128:(tt + 1) * 128], wv, wv[:, kc, hg * 512:(hg + 1) * 512], kc == 0, kc == 3)
                    k.copy(k.evac_eng(), vst, vst[:, :, tt, :], p_, p_[:, :].rearrange("p (h d) -> p h d", h=4))
                for hh in range(4):
                    h = hg * 4 + hh
                    k.store(vst, Vs[h * 128:(h + 1) * 128, :], vst[:, hh, :, :].rearrange("p c d -> p (c d)"))
            k.barrier()

    def phase_D(self):
        k, nc = self.k, self.nc
        with ExitStack() as st:
            onesf = k.sb(st, [128, 128], F32, "onesf")
            k.op(k.dve, [], [onesf], lambda: nc.vector.memset(onesf[:], 1.0))
            acc = [k.sb(st, [128, 512], F32, "lacc") for _ in range(2)]
            kr = k.sb(st, [128, S], BF16, "kr")
            k.op(k.dve, [], [kr], lambda: nc.vector.memset(kr[64:128, :], 0.0))
            k.load(kr, kr[0:64, :], self.scr["krT"][:, :])
            kn = [k.sb(st, [128, S], BF16, "kn") for _ in range(2)]
            vv = [k.sb(st, [128, 32, 128], BF16, "vv") for _ in range(2)]
            qn = [k.sb(st, [128, NOWN], BF16, "qn") for _ in range(2)]
            qr = [k.sb(st, [128, NOWN], BF16, "qr") for _ in range(2)]
            for q_ in qr:
                k.op(k.dve, [], [q_], lambda: nc.vector.memset(q_[64:128, :], 0.0))
            pT = [k.sb(st, [128, 512], BF16, "pT") for _ in range(4)]
            rl = [k.sb(st, [128, 512], F32, "rl") for _ in range(2)]
            ostg = [k.sb(st, [128, NOWN], BF16, "ostg") for _ in range(2)]
            ps_s = [k.ps(st, [128, 512], F32, "pS") for _ in range(3)]
            ps_o = [k.ps(st, [128, 512], F32, "pO") for _ in range(2)]
            ps_l = [k.ps(st, [128, 512], F32, "pL") for _ in range(2)]
            NKC = 32

            def hloads(h):
                kn_, vv_, qn_, qr_ = kn[h % 2], vv[h % 2], qn[h % 2], qr[h % 2]
                k.load(kn_, kn_[:, :], self.scr["knT"][h * 128:(h + 1) * 128, :])
                k.load(vv_, vv_[:].rearrange("p c d -> p (c d)"), self.scr["Vs"][h * 128:(h + 1) * 128, :])
                k.load(qn_, qn_[:, :], self.scr["qnT"][h * 128:(h + 1) * 128, :])
                k.load(qr_, qr_[0:64, :], self.scr["qrT"][h * 64:(h + 1) * 64, :])
            items = []
            for h in range(HEADS):
                for bi, (t0, tw) in enumerate(OWN_BLOCKS):
                    items.append((h, bi, t0, tw))
            stream = [(n, kc) for n in range(len(items)) for kc in range(NKC)]

            def qk(n, kc):
                h, bi, t0, tw = items[n]
                kn_, qn_, qr_ = kn[h % 2], qn[h % 2], qr[h % 2]
                p_ = ps_s[(n * NKC + kc) % 3]
                k.mm(p_, p_[:, :tw], kn_, kn_[:, kc * 128:(kc + 1) * 128], qn_, qn_[:, t0:t0 + tw], True, False)
                k.mm(p_, p_[:, :tw], kr, kr[:, kc * 128:(kc + 1) * 128], qr_, qr_[:, t0:t0 + tw], False, True)

            def pv(n, kc):
                h, bi, t0, tw = items[n]
                vv_ = vv[h % 2]
                po = ps_o[n % 2]
                p_ = ps_s[(n * NKC + kc) % 3]
                e_ = pT[kc % 4]
                k.actf(e_, e_[:, :tw], p_, p_[:, :tw], AF.Exp, scale=SCALE)
                k.mm(po, po[:, :tw], vv_, vv_[:, kc, :], e_, e_[:, :tw], kc == 0, kc == NKC - 1)
                ac = acc[kc % 2]
                E_, eng = (k.dve, nc.vector) if kc % 2 == 0 else (k.pool, nc.gpsimd)
                if kc < 2:
                    k.op(E_, [e_], [ac], lambda: eng.tensor_copy(out=ac[:, :tw], in_=e_[:, :tw]))
                else:
                    k.op(E_, [ac, e_], [ac], lambda: eng.tensor_tensor(
                        out=ac[:, :tw], in0=ac[:, :tw], in1=e_[:, :tw], op=ALU.add))

            def fin(n):
                h, bi, t0, tw = items[n]
                po, pl = ps_o[n % 2], ps_l[n % 2]
                og = ostg[h % 2]
                k.mm(pl, pl[:, :tw], onesf, onesf[:], acc[0], acc[0][:, :tw], True, False)
                k.mm(pl, pl[:, :tw], onesf, onesf[:], acc[1], acc[1][:, :tw], False, True)
                r_ = rl[n % 2]
                k.op(k.dve, [pl], [r_], lambda: nc.vector.reciprocal(out=r_[:, :tw], in_=pl[:, :tw]))
                k.op(k.dve, [po, r_], [og], lambda: nc.vector.tensor_tensor(
                    out=og[:, t0:t0 + tw], in0=po[:, :tw], in1=r_[:, :tw], op=ALU.mult))
                if bi == len(OWN_BLOCKS) - 1:
                    k.store(og, self.scr["oT"][h * 128:(h + 1) * 128, :], og[:, :])

            hloads(0)
            hloads(1)
            qk(*stream[0])
            qk(*stream[1])
            for j, (n, kc) in enumerate(stream):
                h, bi, t0, tw = items[n]
                if kc == 0 and bi == 0 and h >= 1 and h + 1 < HEADS:
                    hloads(h + 1)
                if j + 2 < len(stream):
                    qk(*stream[j + 2])
                pv(n, kc)
                if kc == NKC - 1:
                    fin(n)
            k.barrier()

    def phase_E(self):
        k, nc = self.k, self.nc
        with ExitStack() as st:
            xT = k.sb(st, [128, KC, NOWN], BF16, "oTres")
            oT = self.scr["oT"].rearrange("(kc p) t -> p kc t", p=128)
            for kc in range(KC):
                k.load(xT, xT[:, kc, :], oT[:, kc, :])
            wt = [k.sb(st, [128, KC, 512], BF16, "wt") for _ in range(2)]
            psums = [k.ps(st, [128, 512], F32, "pE") for _ in range(6)]
            grow = [k.sb(st, [128, NOWN], BF16, "grow") for _ in range(2)]
            stg = [k.sb(st, [128, NOWN], F32, "estg") for _ in range(2)]
            state = {"i": 0}
            gT = self.scr["gT"]
            gao = self.scr["gaoT"]

            def evac(p_, f0, fw, t0, tw, last):
                g_ = grow[state["i"] % 2]
                s_ = stg[state["i"] % 2]
                if t0 == 0:
                    k.load(g_, g_[:, :], gT[f0:f0 + 128, :])
                k.op(k.dve, [p_, g_], [s_], lambda: nc.vector.tensor_tensor(
                    out=s_[:, t0:t0 + tw], in0=p_[:, :tw], in1=g_[:, t0:t0 + tw], op=ALU.mult))
                if last:
                    k.store(s_, gao[f0:f0 + 128, :], s_[:, :])
                    state["i"] += 1
            self.proj_fm(st, self.inp["w_oattn"], 2048, xT, KC, OWN_BLOCKS, evac, wt, psums, cast=True)
            k.barrier()

    def phase_F(self):
        k, nc = self.k, self.nc
        with ExitStack() as st:
            ident = k.sb(st, [128, 128], F32, "ident")
            identb = k.sb(st, [128, 128], BF16, "identb")
            k.load(ident, ident[:], self.inp["c_ident"])
            k.copy(k.dve, identb, identb[:], ident, ident[:])
            cw = k.sb(st, [128, 240], F32, "cw")
            cb = k.sb(st, [128, 48], F32, "cb")
            cbrow = k.sb(st, [128, 5120], F32, "cbrow")
            k.load(cw, cw[:], self.inp["cw"])
            k.load(cb, cb[:], self.inp["cb"])
            k.load(cbrow, cbrow[:], self.inp["cb_row"][:, 0:5120].partition_broadcast(128))
            xc = [k.sb(st, [128, 4, S + 4], BF16, "xc") for _ in range(2)]
            dg = [k.sb(st, [128, 4, 5, 128], BF16, "dg") for _ in range(2)]
            tmp = [k.sb(st, [128, 512], F32, "ctmp") for _ in range(2)]
            stg = [k.sb(st, [128, 512], BF16, "cstg") for _ in range(3)]
            psums = [k.ps(st, [128, 512], F32, "pF") for _ in range(6)]
            xbcT = self.scr["xbcT"]
            for x_ in xc:
                k.op(k.dve, [], [x_], lambda: nc.vector.memset(x_[:, :, 0:2], 0.0))
                k.op(k.dve, [], [x_], lambda: nc.vector.memset(x_[:, :, S + 2:S + 4], 0.0))
            cnt = 0
            def ldcg(cg):
                x_, d_ = xc[cg % 2], dg[cg % 2]
                for q in range(4):
                    cc = cg * 4 + q
                    k.load(x_, x_[:, q, 2:S + 2], xbcT[cc * 128:(cc + 1) * 128, :])
                    for j in range(5):
                        k.op(k.dve, [identb, cw], [d_], lambda: nc.vector.tensor_scalar(
                            out=d_[:, q, j, :], in0=identb[:], scalar1=cw[:, cc * 5 + j:cc * 5 + j + 1], scalar2=None, op0=ALU.mult))
            ldcg(0)
            for cg in range(10):
                x_, d_ = xc[cg % 2], dg[cg % 2]
                if cg + 1 < 10:
                    ldcg(cg + 1)
                for tt in range(32):
                    p_ = psums[cnt % 6]
                    for q in range(4):
                        for j in range(5):
                            k.mm(p_, p_[:, q * 128:(q + 1) * 128], x_, x_[:, q, tt * 128 + j:tt * 128 + j + 128],
                                 d_, d_[:, q, j, :], j == 0, j == 4)
                    t_ = tmp[cnt % 2]
                    s_ = stg[cnt % 3]
                    cnt += 1
                    k.op(k.dve, [p_, cbrow], [t_], lambda: nc.vector.tensor_tensor(
                        out=t_[:], in0=p_[:], in1=cbrow[:, cg * 512:(cg + 1) * 512], op=ALU.add))
                    k.actf(s_, s_[:], t_, t_[:], AF.Silu)
                    if cg < 8:
                        k.store(s_, self.scr["xs_tok"][tt * 128:(tt + 1) * 128, cg * 512:(cg + 1) * 512], s_[:])
                    else:
                        k.store(s_, self.scr["B_tok"][tt * 128:(tt + 1) * 128, (cg - 8) * 512:(cg - 7) * 512], s_[:])
            xf = [k.sb(st, [128, NOWN + 4], BF16, "xf") for _ in range(2)]
            dgf = [k.sb(st, [128, 5, 128], BF16, "dgf") for _ in range(2)]
            rst = [k.sb(st, [128, NOWN], BF16, "rst") for _ in range(2)]
            for x_ in xf:
                k.op(k.dve, [], [x_], lambda: nc.vector.memset(x_[:, 0:2], 0.0))
            for i, cc in enumerate(range(32, 48)):
                x_, d_, r_ = xf[i % 2], dgf[i % 2], rst[i % 2]
                k.load(x_, x_[:, 2:NOWN + 4], xbcT[cc * 128:(cc + 1) * 128, 0:NOWN + 2])
                for j in range(5):
                    k.op(k.dve, [identb, cw], [d_], lambda: nc.vector.tensor_scalar(
                        out=d_[:, j, :], in0=identb[:], scalar1=cw[:, cc * 5 + j:cc * 5 + j + 1], scalar2=None, op0=ALU.mult))
                for (t0, tw) in OWN_BLOCKS:
                    p_ = psums[cnt % 6]
                    cnt += 1
                    for j in range(5):
                        k.mm(p_, p_[:, :tw], d_, d_[:, j, :], x_, x_[:, t0 + j:t0 + j + tw], j == 0, j == 4)
                    k.op(k.act, [p_, cb], [r_], lambda: nc.scalar.activation(
                        out=r_[:, t0:t0 + tw], in_=p_[:, :tw], func=AF.Silu, bias=cb[:, cc:cc + 1]))
                k.store(r_, self.scr["BCT"][(cc - 32) * 128:(cc - 31) * 128, :], r_[:, :])
            k.barrier()

    def phase_G(self, d):
        k, nc = self.k, self.nc
        with ExitStack() as st:
            nchk = NCH_OWN if d == 0 else 32
            tri = k.sb(st, [128, 128], F32, "tri")
            negm = k.sb(st, [128, 128], F32, "negm")
            negmb = k.sb(st, [128, 128], BF16, "negmb")
            ident = k.sb(st, [128, 128], F32, "ident")
            identb = k.sb(st, [128, 128], BF16, "identb")
            onesf = k.sb(st, [128, 128], F32, "onesf")
            one_c = k.sb(st, [128, 1], F32, "one_c")
            sel = k.sb(st, [128, 64, 128], BF16, "sel")
            k.load(tri, tri[:], self.inp["c_tri"][:, d * 128:(d + 1) * 128])
            k.load(negm, negm[:], self.inp["c_negm"][:, d * 128:(d + 1) * 128])
            k.load(ident, ident[:], self.inp["c_ident"])
            k.copy(k.dve, negmb, negmb[:], negm, negm[:])
            k.copy(k.dve, identb, identb[:], ident, ident[:])
            k.op(k.dve, [], [onesf], lambda: nc.vector.memset(onesf[:], 1.0))
            k.op(k.dve, [], [one_c], lambda: nc.vector.memset(one_c[:], 1.0))
            k.op(k.dve, [ident], [sel], lambda: nc.vector.tensor_copy(
                out=sel[0:64], in_=ident[0:64, 0:64].unsqueeze(2).to_broadcast([64, 64, 128])))
            k.op(k.dve, [], [sel], lambda: nc.vector.memset(sel[64:128], 0.0))
            dt_t = k.sb(st, [128, nchk, 64], F32, "dt_t")
            a_t = k.sb(st, [128, nchk, 64], F32, "a_t")
            dtb_b = k.sb(st, [128, 64], F32, "dtb_b")
            negA = k.sb(st, [128, 64], F32, "negA")
            k.load(dt_t, dt_t[:], self.scr["dtraw"][0:nchk * 128, d * 64:(d + 1) * 64].rearrange("(c p) h -> p c h", p=128))
            k.load(dtb_b, dtb_b[:], self.inp["dtb"][:, d * 64:(d + 1) * 64].partition_broadcast(128))
            k.load(negA, negA[:], self.inp["alog"][:, d * 64:(d + 1) * 64].partition_broadcast(128))
            k.actf(negA, negA[:], negA, negA[:], AF.Exp)
            k.op(k.dve, [negA], [negA], lambda: nc.vector.tensor_scalar(
                out=negA[:], in0=negA[:], scalar1=-1.0, scalar2=None, op0=ALU.mult))
            k.op(k.dve, [dt_t, dtb_b], [dt_t], lambda: nc.vector.tensor_tensor(
                out=dt_t[:], in0=dt_t[:], in1=dtb_b[:].unsqueeze(1).to_broadcast([128, nchk, 64]), op=ALU.add))
            k.actf(dt_t, dt_t[:], dt_t, dt_t[:], AF.Exp)
            k.op(k.act, [dt_t, one_c], [dt_t], lambda: nc.scalar.activation(
                out=dt_t[:], in_=dt_t[:], func=AF.Ln, bias=one_c[:, 0:1]))
            k.op(k.dve, [dt_t, negA], [a_t], lambda: nc.vector.tensor_tensor(
                out=a_t[:], in0=dt_t[:], in1=negA[:].unsqueeze(1).to_broadcast([128, nchk, 64]), op=ALU.mult))
            lndt = k.sb(st, [128, NCH_OWN, 64], F32, "lndt")
            k.actf(lndt, lndt[:], dt_t, dt_t[:, 0:NCH_OWN, :], AF.Ln)
            H = k.sb(st, [128, DI], F32, "H")
            k.op(k.dve, [], [H], lambda: nc.vector.memset(H[:], 0.0))
            prevb = k.sb(st, [128, DI], BF16, "prevb")
            xs_c = [k.sb(st, [128, DI], BF16, "xs_c") for _ in range(2)]
            B_c = [k.sb(st, [128, 1024], BF16, "B_c") for _ in range(2)]
            bct = [k.sb(st, [128, 16, 128], BF16, "bct") for _ in range(2)]
            xd_l = [None, None]
            xdd = k.sb(st, [128, DI], BF16, "xdd")
            dbl = lambda shape, dt, nm: [k.sb(st, shape, dt, nm) for _ in range(2)]
            cs_sb_l = dbl([128, 64], F32, "cs_sb")
            ncs_l = dbl([128, 64], F32, "ncs")
            E_sb_l = dbl([128, 64], F32, "E_sb")
            dte_l = dbl([128, 64], F32, "dte")
            cd_sb_l = dbl([128, 64], F32, "cd_sb")
            w1_l = dbl([128, 64], F32, "w1")
            cshl_l = dbl([128, 2, 128], BF16, "cshl")
            for c_ in cshl_l:
                k.op(k.dve, [], [c_], lambda: nc.vector.memset(c_[64:128], 0.0))
            nbhl_l = dbl([128, 2, 128], BF16, "nbhl")
            for c_ in nbhl_l:
                k.op(k.dve, [], [c_], lambda: nc.vector.memset(c_[64:128], 0.0))
            dec8 = [k.sb(st, [128, 8, 128], F32, "dec8") for _ in range(2)]
            mt8 = [k.sb(st, [128, 8, 128], BF16, "mt8") for _ in range(2)]
            ych_l = [k.sb(st, [128, DI], F32, "ych") for _ in range(2 if d == 0 else 1)]
            psm = k.ps(st, [128, 512], F32, "psm")
            pcbx = st.enter_context(nc.psum_tensor("g_pcbx_%d" % d, [128, 256], F32))
            pcb = [T(pcbx, "pcb0"), T(pcbx, "pcb1")]
            pcb_ap = [pcbx[:, 0:128], pcbx[:, 128:256]]
            k.phase_tiles += pcb
            pseg = [k.ps(st, [128, 512], F32, "pseg") for _ in range(2)]
            pyd = k.ps(st, [128, 512], F32, "pyd")
            pyo = k.ps(st, [128, 512], F32, "pyo")
            pst_l = [k.ps(st, [128, 512], F32, "pst") for _ in range(2)]
            if d == 1:
                yf_t = k.sb(st, [128, DI], F32, "yf_t")
                sz_t = k.sb(st, [128, DI], BF16, "sz_t")
                yn_t = k.sb(st, [128, DI], BF16, "yn_t")
                nwb = k.sb(st, [128, DI], F32, "nwb")
                dsk = k.sb(st, [128, 64], F32, "dsk")
                ss = k.sb(st, [128, 1], F32, "ss")
                rstd = k.sb(st, [128, 1], F32, "rstd")
                k.load(nwb, nwb[:], self.inp["nw_ssd"].partition_broadcast(128))
                k.load(dsk, dsk[:], self.inp["dsk"].partition_broadcast(128))
            other = list(range(31, 16, -1)) if d == 1 else []
            own = list(range(17)) if d == 0 else list(range(16, -1, -1))
            seq = [(c, False) for c in other] + [(c, True) for c in own]
            def prep(it):
                c, is_own = seq[it]
                pb = it % 2
                xd, cs_sb, ncs, E_sb, dte, cd_sb, w1, cshl = (xd_l[pb], cs_sb_l[pb], ncs_l[pb], E_sb_l[pb], dte_l[pb],
                                                              cd_sb_l[pb], w1_l[pb], cshl_l[pb])
                xs_, Bc_ = xs_c[it % 2], B_c[it % 2]
                k.load(xs_, xs_[:], self.scr["xs_tok"][c * 128:(c + 1) * 128, :])
                k.load(Bc_, Bc_[:], self.scr["B_tok"][c * 128:(c + 1) * 128, :])
                a_c = a_t[:, c, :]
                k.mm(psm, psm[:, 0:64], tri, tri[:], a_t, a_c, True, True)
                k.mm(psm, psm[:, 64:128], onesf, onesf[:], a_t, a_c, True, True)
                k.copy(k.act, cs_sb, cs_sb[:], psm, psm[:, 0:64])
                k.op(k.dve, [psm, cs_sb], [dte], lambda: nc.vector.tensor_tensor(
                    out=dte[:], in0=psm[:, 64:128], in1=cs_sb[:], op=ALU.subtract))
                k.actf(dte, dte[:], dte, dte[:], AF.Exp)
                k.actf(cd_sb, cd_sb[:], psm, psm[:, 64:128], AF.Exp)
                k.op(k.dve, [dt_t, dte], [w1], lambda: nc.vector.tensor_tensor(
                    out=w1[:], in0=dt_t[:, c, :], in1=dte[:], op=ALU.mult))
                if is_own:
                    bc_ = bct[it % 2]
                    k.load(bc_, bc_[:], self.scr["BCT"][:, c * 128:(c + 1) * 128].rearrange("(j p) t -> p j t", p=128))
                    k.op(k.dve, [lndt, psm], [ncs], lambda: nc.vector.tensor_tensor(
                        out=ncs[:], in0=lndt[:, c, :], in1=psm[:, 0:64], op=ALU.subtract))
                    k.actf(E_sb, E_sb[:], psm, psm[:, 0:64], AF.Exp)
                    k.op(k.pe, [cs_sb, ident], [psm], lambda: nc.tensor.transpose(
                        out=psm[0:64, 128:256], in_=cs_sb[:], identity=ident[:]))
                    k.copy(k.act, cshl, cshl[0:64, 0, :], psm, psm[0:64, 128:256])
                    k.op(k.dve, [psm, cshl], [cshl], lambda: nc.vector.tensor_tensor(
                        out=cshl[0:64, 1, :], in0=psm[0:64, 128:256], in1=cshl[0:64, 0, :], op=ALU.subtract))
                    nbhl = nbhl_l[pb]
                    k.op(k.pe, [ncs, ident], [psm], lambda: nc.tensor.transpose(
                        out=psm[0:64, 256:384], in_=ncs[:], identity=ident[:]))
                    k.copy(k.act, nbhl, nbhl[0:64, 0, :], psm, psm[0:64, 256:384])
                    k.op(k.dve, [psm, nbhl], [nbhl], lambda: nc.vector.tensor_tensor(
                        out=nbhl[0:64, 1, :], in0=psm[0:64, 256:384], in1=nbhl[0:64, 0, :], op=ALU.subtract))

            def body(it):
                c, is_own = seq[it]
                pb = it % 2
                xd, cs_sb, ncs, E_sb, dte, cd_sb, w1, cshl = (xd_l[pb], cs_sb_l[pb], ncs_l[pb], E_sb_l[pb], dte_l[pb],
                                                              cd_sb_l[pb], w1_l[pb], cshl_l[pb])
                xs_, Bc_ = xs_c[it % 2], B_c[it % 2]
                bc_ = bct[it % 2]
                ych = ych_l[it % len(ych_l)]
                k.op(k.pool, [xs_, w1], [xdd], lambda: nc.gpsimd.tensor_tensor(
                    out=xdd[:].rearrange("p (h q) -> p h q", q=64), in0=xs_[:].rearrange("p (h q) -> p h q", q=64),
                    in1=w1[:].unsqueeze(2).to_broadcast([128, 64, 64]), op=ALU.mult))
                if is_own:
                    if d == 1:
                        k.load(yf_t, yf_t[:], self.scr["yf"][c * 128:(c + 1) * 128, :])
                        k.load(sz_t, sz_t[:], self.scr["sz"][c * 128:(c + 1) * 128, :])
                    k.copy(k.act, prevb, prevb[:], H, H[:])

                    def seg(g):
                        pc_ = pcb[g % 2]
                        k.mm(pc_, pcb_ap[g % 2], bc_, bc_[:, g, :], bc_, bc_[:, 8 + g, :], True, True)
                        d8, m8 = dec8[g % 2], mt8[g % 2]
                        nbhl = nbhl_l[pb]
                        for hh in range(8):
                            h = g * 8 + hh
                            pg_ = pseg[hh // 4]
                            reg = pg_[:, (hh % 4) * 128:(hh % 4 + 1) * 128]
                            k.mm(pg_, reg, sel, sel[:, h, :], cshl, cshl[:, 0, :], True, False)
                            k.mm(pg_, reg, sel, sel[:, h, :], cshl, cshl[:, 1, :], False, False)
                            k.mm(pg_, reg, nbhl, nbhl[:, 0, :], sel, sel[:, h, :], False, False)
                            k.mm(pg_, reg, nbhl, nbhl[:, 1, :], sel, sel[:, h, :], False, False)
                            k.mm(pg_, reg, identb, identb[:], negmb, negmb[:], False, True)
                            if hh % 4 == 3:
                                hb = hh // 4
                                k.op(k.act, [pg_], [d8], lambda: nc.scalar.activation(
                                    out=d8[:, hb * 4:(hb + 1) * 4, :], in_=pg_[:].rearrange("p (a b) -> p a b", a=4),
                                    func=AF.Exp))

                    def rest(g):
                        pc_ = pcb[g % 2]
                        d8, m8 = dec8[g % 2], mt8[g % 2]
                        k.op(k.dve, [d8, pc_], [m8], lambda: nc.vector.tensor_tensor(
                            out=m8[:], in0=d8[:], in1=pcb_ap[g % 2].unsqueeze(1).to_broadcast([128, 8, 128]), op=ALU.mult))
                        for hh in range(8):
                            h = g * 8 + hh
                            k.mm(pyd, pyd[:, hh * 64:(hh + 1) * 64], m8, m8[:, hh, :], xs_, xs_[:, h * 64:(h + 1) * 64], True, True)
                        k.mm(pyo, pyo[:], bc_, bc_[:, 8 + g, :], prevb, prevb[:, g * 512:(g + 1) * 512], True, True)
                        yg = ych[:, g * 512:(g + 1) * 512]
                        k.op(k.dve, [pyo, E_sb], [ych], lambda: nc.vector.tensor_tensor(
                            out=yg.rearrange("p (h q) -> p h q", q=64), in0=pyo[:].rearrange("p (h q) -> p h q", q=64),
                            in1=E_sb[:, g * 8:(g + 1) * 8].unsqueeze(2).to_broadcast([128, 8, 64]), op=ALU.mult))
                        k.op(k.dve, [ych, pyd], [ych], lambda: nc.vector.tensor_tensor(
                            out=yg, in0=yg, in1=pyd[:], op=ALU.add))
                    seg(0)
                    for g in range(8):
                        if g + 1 < 8:
                            seg(g + 1)
                        rest(g)
                k.op(k.pool, [H, cd_sb], [H], lambda: nc.gpsimd.tensor_tensor(
                    out=H[:].rearrange("p (h q) -> p h q", q=64), in0=H[:].rearrange("p (h q) -> p h q", q=64),
                    in1=cd_sb[:].unsqueeze(2).to_broadcast([128, 64, 64]), op=ALU.mult))
                for g in range(8):
                    pst = pst_l[g % 2]
                    k.mm(pst, pst[:], Bc_, Bc_[:, g * 128:(g + 1) * 128], xdd, xdd[:, g * 512:(g + 1) * 512], True, True)
                    Hg = H[:, g * 512:(g + 1) * 512]
                    k.op(k.dve, [H, pst], [H], lambda: nc.vector.tensor_tensor(out=Hg, in0=Hg, in1=pst[:], op=ALU.add))
                if not is_own:
                    return
                if d == 0:
                    k.store(ych, self.scr["yf"][c * 128:(c + 1) * 128, :], ych[:])
                else:
                    k.op(k.pool, [ych, yf_t], [ych], lambda: nc.gpsimd.tensor_tensor(out=ych[:], in0=ych[:], in1=yf_t[:], op=ALU.add))
                    k.op(k.pool, [xs_, dsk], [yf_t], lambda: nc.gpsimd.tensor_tensor(
                        out=yf_t[:].rearrange("p (h q) -> p h q", q=64), in0=xs_[:].rearrange("p (h q) -> p h q", q=64),
                        in1=dsk[:].unsqueeze(2).to_broadcast([128, 64, 64]), op=ALU.mult))
                    k.op(k.dve, [ych, yf_t], [ych], lambda: nc.vector.tensor_tensor(out=ych[:], in0=ych[:], in1=yf_t[:], op=ALU.add))
                    k.op(k.dve, [ych, sz_t], [ych], lambda: nc.vector.tensor_tensor(out=ych[:], in0=ych[:], in1=sz_t[:], op=ALU.mult))
                    k.op(k.act, [ych], [yf_t, ss], lambda: nc.scalar.activation(
                        out=yf_t[:], in_=ych[:], func=AF.Square, accum_out=ss[:]))
                    self.rstd_from_ss(rstd, ss, DI)
                    k.op(k.dve, [ych, rstd, nwb], [yn_t], lambda: nc.vector.scalar_tensor_tensor(
                        out=yn_t[:], in0=ych[:], scalar=rstd[:, 0:1], in1=nwb[:], op0=ALU.mult, op1=ALU.mult))
                    k.store(yn_t, self.scr["yn"][c * 128:(c + 1) * 128, :], yn_t[:])

            prep(0)
            for it in range(len(seq)):
                if it + 1 < len(seq):
                    prep(it + 1)
                body(it)
            k.barrier()

    def phase_H(self):
        k, nc = self.k, self.nc
        with ExitStack() as st:
            ident = k.sb(st, [128, 128], F32, "ident")
            identb = k.sb(st, [128, 128], BF16, "identb")
            k.load(ident, ident[:], self.inp["c_ident"])
            k.copy(k.dve, identb, identb[:], ident, ident[:])
            ynt = [k.sb(st, [128, DI], BF16, "ynt") for _ in range(2)]
            xTb = [k.sb(st, [128, 32, 512], BF16, "xTb") for _ in range(2)]
            wt = [k.sb(st, [128, 32, 512], BF16, "wtH") for _ in range(2)]
            pt = [k.ps(st, [128, 8, 128], BF16, "ptH") for _ in range(2)]
            psums = [k.ps(st, [128, 512], F32, "pH") for _ in range(5)]
            gsr = [k.sb(st, [128, 512], BF16, "gsr") for _ in range(3)]
            gar = [k.sb(st, [128, 512], F32, "gar") for _ in range(3)]
            tmp = [k.sb(st, [128, 512], F32, "tmpH") for _ in range(3)]
            mst = [k.sb(st, [128, 512], BF16, "mst") for _ in range(3)]
            gT, gao, mixT = self.scr["gT"], self.scr["gaoT"], self.scr["mixT"]
            cnt = {"i": 0, "p": 0}

            def tr(bi):
                t0, tw = OWN_BLOCKS[bi]
                xb = xTb[bi % 2]
                for ti in range(tw // 128):
                    tt = t0 // 128 + ti
                    y_ = ynt[tt % 2]
                    k.load(y_, y_[:], self.scr["yn"][tt * 128:(tt + 1) * 128, :])
                    for grp in range(4):
                        p_ = pt[cnt["p"] % 2]
                        cnt["p"] += 1
                        for j in range(8):
                            kc = grp * 8 + j
                            k.op(k.pe, [y_, identb], [p_], lambda: nc.tensor.transpose(
                                out=p_[:, j, :], in_=y_[:, kc * 128:(kc + 1) * 128], identity=identb[:]))
                        k.copy(k.evac_eng(), xb, xb[:, grp * 8:(grp + 1) * 8, ti * 128:(ti + 1) * 128], p_, p_[:])

            def mmb(bi):
                t0, tw = OWN_BLOCKS[bi]
                xb = xTb[bi % 2]

                def evac(p_, f0, fw, t0_, tw_, last, t0=t0, tw=tw):
                    i = cnt["i"] % 3
                    cnt["i"] += 1
                    k.load(gsr[i], gsr[i][:, :tw], gT[2048 + f0:2048 + f0 + 128, t0:t0 + tw])
                    k.load(gar[i], gar[i][:, :tw], gao[f0:f0 + 128, t0:t0 + tw])
                    k.op(k.dve, [p_, gsr[i]], [tmp[i]], lambda: nc.vector.tensor_tensor(
                        out=tmp[i][:, :tw], in0=p_[:, :tw], in1=gsr[i][:, :tw], op=ALU.mult))
                    k.op(k.pool, [tmp[i], gar[i]], [mst[i]], lambda: nc.gpsimd.tensor_tensor(
                        out=mst[i][:, :tw], in0=tmp[i][:, :tw], in1=gar[i][:, :tw], op=ALU.add))
                    k.store(mst[i], mixT[f0:f0 + 128, t0:t0 + tw], mst[i][:, :tw])
                self.proj_fm(st, self.scr["b_w_ossd"], 2048, xb, 32, [(0, tw)], evac, wt, psums)
            tr(0)
            for bi in range(len(OWN_BLOCKS)):
                if bi + 1 < len(OWN_BLOCKS):
                    tr(bi + 1)
                mmb(bi)
            k.barrier()

    def phase_I(self):
        k, nc = self.k, self.nc
        with ExitStack() as st:
            ident = k.sb(st, [128, 128], F32, "ident")
            identb = k.sb(st, [128, 128], BF16, "identb")
            k.load(ident, ident[:], self.inp["c_ident"])
            k.copy(k.dve, identb, identb[:], ident, ident[:])
            wout = k.sb(st, [128, KC, D], BF16, "wout")
            wo = self.inp["w_out"].rearrange("(kc p) f -> p kc f", p=128)
            for kc in range(KC):
                self.wload(wout, wout[:, kc, :], wo[:, kc, :], True)
            nwb = k.sb(st, [128, D], F32, "nwb")
            k.load(nwb, nwb[:], self.inp["nw_ffn"].partition_broadcast(128))
            mixt = [k.sb(st, [128, KC, 128], BF16, "mixt") for _ in range(2)]
            xin = [k.sb(st, [128, D], F32, "xin") for _ in range(2)]
            h1t = [k.sb(st, [128, D], F32, "h1t") for _ in range(2)]
            xn = [k.sb(st, [128, D], BF16, "xn") for _ in range(2)]
            junk = k.sb(st, [128, D], BF16, "junk")
            ss = [k.sb(st, [128, 1], F32, "ss") for _ in range(2)]
            rstd = [k.sb(st, [128, 1], F32, "rstd") for _ in range(2)]
            stage = [k.sb(st, [128, KC, 512], BF16, "stage") for _ in range(2)]
            psums = [k.ps(st, [128, 512], F32, "pI") for _ in range(4)]
            pt = [k.ps(st, [128, 8, 128], BF16, "ptI") for _ in range(4)]
            mixT = self.scr["mixT"].rearrange("(kc p) t -> p kc t", p=128)
            n2T = self.scr["n2T"].rearrange("(kc p) t -> p kc t", p=128)
            def front(tt):
                m_, xi, h_, xo = mixt[tt % 2], xin[tt % 2], h1t[tt % 2], xn[tt % 2]
                nonlocal_cnt = front.cnt
                k.load(m_, m_[:], mixT[:, :, tt * 128:(tt + 1) * 128])
                k.load(xi, xi[:], self.inp["xl"][tt * 128:(tt + 1) * 128, :])
                for ob in range(4):
                    p_ = psums[front.cnt % 4]
                    front.cnt += 1
                    for kc in range(KC):
                        k.mm(p_, p_[:], m_, m_[:, kc, :], wout, wout[:, kc, ob * 512:(ob + 1) * 512], kc == 0, kc == KC - 1)
                    k.op(k.dve, [p_, xi], [h_], lambda: nc.vector.tensor_tensor(
                        out=h_[:, ob * 512:(ob + 1) * 512], in0=p_[:], in1=xi[:, ob * 512:(ob + 1) * 512], op=ALU.add))
                k.store(h_, self.scr["h1"][tt * 128:(tt + 1) * 128, :], h_[:])
                s_, r_ = ss[tt % 2], rstd[tt % 2]
                k.op(k.act, [h_], [junk, s_], lambda: nc.scalar.activation(
                    out=junk[:], in_=h_[:], func=AF.Square, accum_out=s_[:]))
                self.rstd_from_ss(r_, s_, D)
                k.op(k.dve, [h_, r_, nwb], [xo], lambda: nc.vector.scalar_tensor_tensor(
                    out=xo[:], in0=h_[:], scalar=r_[:, 0:1], in1=nwb[:], op0=ALU.mult, op1=ALU.mult))

            def back(tt):
                g, i = tt // 4, tt % 4
                xo = xn[tt % 2]
                sg = stage[g % 2]
                for half in range(2):
                    p_ = pt[(2 * tt + half) % 4]
                    for j in range(8):
                        kc = half * 8 + j
                        k.op(k.pe, [xo, identb], [p_], lambda: nc.tensor.transpose(
                            out=p_[:, j, :], in_=xo[:, kc * 128:(kc + 1) * 128], identity=identb[:]))
                    k.copy(k.evac_eng(), sg, sg[:, half * 8:(half + 1) * 8, i * 128:(i + 1) * 128], p_, p_[:])
                if i == 3 or tt == NCH_OWN - 1:
                    w_ = (i + 1) * 128
                    k.store(sg, n2T[:, :, g * 512:g * 512 + w_], sg[:, :, :w_])
            front.cnt = 0
            front(0)
            for tt in range(NCH_OWN):
                if tt + 1 < NCH_OWN:
                    front(tt + 1)
                back(tt)
            k.barrier()

    def phase_J(self):
        k, nc = self.k, self.nc
        NT = NOUT
        with ExitStack() as st:
            n2 = k.sb(st, [128, KC, NOWN], BF16, "n2res")
            n2T = self.scr["n2T"].rearrange("(kc p) t -> p kc t", p=128)
            for kc in range(KC):
                k.load(n2, n2[:, kc, :], n2T[:, kc, :])
            fw = k.sb(st, [128, 264], F32, "fw")
            fb = k.sb(st, [128, 88], F32, "fb")
            k.load(fw, fw[:], self.inp["fw"])
            k.load(fb, fb[:], self.inp["fb"])
            wg = [k.sb(st, [128, KC, 512], BF16, "wg") for _ in range(2)]
            wv = [k.sb(st, [128, KC, 512], BF16, "wv") for _ in range(2)]
            pre = [k.sb(st, [128, NT + 2], F32, "pre") for _ in range(2)]
            acc = [k.sb(st, [128, NT], F32, "acc") for _ in range(2)]
            sg = k.sb(st, [128, NT], F32, "sg")
            ast = [k.sb(st, [128, NT], BF16, "ast") for _ in range(2)]
            psums = [k.ps(st, [128, 512], F32, "pJ") for _ in range(6)]
            for p_ in pre:
                k.op(k.dve, [], [p_], lambda: nc.vector.memset(p_[:, 0:1], 0.0))
            blocks = [(0, 512), (512, 512), (1024, 512), (1536, 512), (2048, 1)]
            wup = self.inp["w_up"].rearrange("(kc p) f -> p kc f", p=128)
            cnt = 0
            def ldw(wi):
                self.wload(wg[wi % 2], wg[wi % 2][:], wup[:, :, wi * 512:(wi + 1) * 512], True)
                self.wload(wv[wi % 2], wv[wi % 2][:], wup[:, :, FFN + wi * 512:FFN + (wi + 1) * 512], True)
            ldw(0)
            for i in range(44):
                wi, q = i // 4, i % 4
                if q == 0 and wi + 1 < 11:
                    ldw(wi + 1)
                for which in range(2):
                    w_ = (wg if which == 0 else wv)[wi % 2]
                    ch = i if which == 0 else 44 + i
                    pr, ac = pre[which], acc[which]
                    for (t0, tw) in blocks:
                        p_ = psums[cnt % 6]
                        cnt += 1
                        for kc in range(KC):
                            k.mm(p_, p_[:, :tw], w_, w_[:, kc, q * 128:(q + 1) * 128], n2, n2[:, kc, t0:t0 + tw], kc == 0, kc == KC - 1)
                        k.copy(k.act, pr, pr[:, 1 + t0:1 + t0 + tw], p_, p_[:, :tw])
                    k.op(k.dve, [pr, fw, fb], [ac], lambda: nc.vector.tensor_scalar(
                        out=ac[:], in0=pr[:, 0:NT], scalar1=fw[:, ch * 3:ch * 3 + 1], scalar2=fb[:, ch:ch + 1],
                        op0=ALU.mult, op1=ALU.add))
                    for j in (1, 2):
                        k.op(k.dve, [pr, fw, ac], [ac], lambda: nc.vector.scalar_tensor_tensor(
                            out=ac[:], in0=pr[:, j:j + NT], scalar=fw[:, ch * 3 + j:ch * 3 + j + 1], in1=ac[:],
                            op0=ALU.mult, op1=ALU.add))
                k.actf(sg, sg[:], acc[0], acc[0][:], AF.Silu)
                a_ = ast[i % 2]
                k.op(k.pool, [sg, acc[1]], [a_], lambda: nc.gpsimd.tensor_tensor(
                    out=a_[:], in0=sg[:], in1=acc[1][:], op=ALU.mult))
                k.store(a_, self.scr["aT"][i * 128:(i + 1) * 128, :], a_[:])
            k.barrier()

    def phase_K(self):
        k, nc = self.k, self.nc
        NKF = FFN // 128
        with ExitStack() as st:
            aTb = [k.sb(st, [128, NKF, 512], BF16, "aTb") for _ in range(2)]
            wd = [k.sb(st, [128, NKF, 256], BF16, "wd") for _ in range(2)]
            h1t = [k.sb(st, [128, D], F32, "h1k") for _ in range(4)]
            outt = [k.sb(st, [128, D], F32, "outt") for _ in range(2)]
            nwb = k.sb(st, [128, D], F32, "nwb")
            junk = k.sb(st, [128, D], BF16, "junk")
            ss = [k.sb(st, [128, 1], F32, "ss") for _ in range(2)]
            rstd = [k.sb(st, [128, 1], F32, "rstd") for _ in range(2)]
            psums = [k.ps(st, [128, 512], F32, "pK") for _ in range(6)]
            k.load(nwb, nwb[:], self.inp["nw_fin"].partition_broadcast(128))
            aT = self.scr["aT"].rearrange("(kc p) t -> p kc t", p=128)
            wdn = self.scr["b_w_down"].rearrange("(kc p) f -> p kc f", p=128)
            cnt = 0
            wcnt = 0
            k.load(aTb[0], aTb[0][:], aT[:, :, 0:512])
            for tb in range(NOUT // 512):
                a_ = aTb[tb % 2]
                if tb + 1 < NOUT // 512:
                    k.load(aTb[(tb + 1) % 2], aTb[(tb + 1) % 2][:], aT[:, :, (tb + 1) * 512:(tb + 2) * 512])
                for ti in range(4):
                    tt = tb * 4 + ti
                    k.load(h1t[ti], h1t[ti][:], self.scr["h1"][tt * 128:(tt + 1) * 128, :])
                for ob in range(8):
                    w_ = wd[wcnt % 2]
                    wcnt += 1
                    k.load(w_, w_[:], wdn[:, :, ob * 256:(ob + 1) * 256])
                    for ti in range(4):
                        p_ = psums[cnt % 6]
                        cnt += 1
                        for kc in range(NKF):
                            k.mm(p_, p_[:, :256], a_, a_[:, kc, ti * 128:(ti + 1) * 128], w_, w_[:, kc, :], kc == 0, kc == NKF - 1)
                        h_ = h1t[ti]
                        k.op(k.dve, [p_, h_], [h_], lambda: nc.vector.tensor_tensor(
                            out=h_[:, ob * 256:(ob + 1) * 256], in0=p_[:, :256], in1=h_[:, ob * 256:(ob + 1) * 256], op=ALU.add))
                for ti in range(4):
                    tt = tb * 4 + ti
                    h_, o_, s_, r_ = h1t[ti], outt[ti % 2], ss[ti % 2], rstd[ti % 2]
                    k.op(k.act, [h_], [junk, s_], lambda: nc.scalar.activation(
                        out=junk[:], in_=h_[:], func=AF.Square, accum_out=s_[:]))
                    self.rstd_from_ss(r_, s_, D)
                    k.op(k.dve, [h_, r_, nwb], [o_], lambda: nc.vector.scalar_tensor_tensor(
                        out=o_[:], in0=h_[:], scalar=r_[:, 0:1], in1=nwb[:], op0=ALU.mult, op1=ALU.mult))
                    k.store(o_, self.yout[tt * 128:(tt + 1) * 128, :], o_[:])
            k.barrier()

    def finish(self):
        k, nc = self.k, self.nc
        k.barrier()

    def build(self):
        self.declare()
        phases = [self.phase_cast, self.phase_A, lambda: self.phase_B(True), lambda: self.phase_B(False),
                  self.phase_C1, self.phase_C2, self.phase_D, self.phase_E,
                  self.phase_F, lambda: self.phase_G(0), lambda: self.phase_G(1),
                  self.phase_H, self.phase_I, self.phase_J, self.phase_K]
        for i, p in enumerate(phases):
            if i > self.upto:
                break
            p()
        self.finish()
        return self.nc


def _consts():
    ident = np.eye(128, dtype=np.float32)
    s = np.arange(128)[:, None]
    l = np.arange(128)[None, :]
    tri_f = (s <= l).astype(np.float32)
    tri_b = (s >= l).astype(np.float32)
    neg_f = np.where(l >= s, 0.0, -30000.0).astype(np.float32)
    neg_b = np.where(l <= s, 0.0, -30000.0).astype(np.float32)
    half = 32
    invf = (10000.0 ** (-np.arange(half, dtype=np.float32) / half)).astype(np.float32)
    invf128 = np.tile(invf, 4).reshape(128, 1)
    sgn = np.tile(np.concatenate([-np.ones(32, np.float32), np.ones(32, np.float32)]), 2).reshape(128, 1)
    return dict(c_ident=ident, c_tri=np.concatenate([tri_f, tri_b], 1), c_negm=np.concatenate([neg_f, neg_b], 1),
                c_invf=invf128, c_sgn=sgn)


def prep_core(inp, c, shared):
    b, flip = c // 2, c % 2
    x = inp["x"][b]
    pos = inp["positions"][b]
    if flip:
        x = x[::-1]
        pos = pos[::-1]
    m = dict(shared)
    m["xl"] = np.ascontiguousarray(x, dtype=np.float32)
    m["pos"] = np.ascontiguousarray(pos, dtype=np.int32).reshape(1, S)
    w_in = inp["w_in"][0]
    dtf, dtb_ = w_in[:, 11328:11392], w_in[:, 11392:11456]
    m["w_dt"] = np.ascontiguousarray(np.concatenate([dtb_, dtf] if flip else [dtf, dtb_], 1))
    cw = inp["ssd_conv_w"][0]
    fw = inp["ffn_conv_w"][0]
    if flip:
        cw = cw[::-1]
        fw = fw[::-1]
    m["cw"] = np.ascontiguousarray(cw.reshape(5, 48, 128).transpose(2, 1, 0).reshape(128, 240))
    m["fw"] = np.ascontiguousarray(fw.reshape(3, 88, 128).transpose(2, 1, 0).reshape(128, 264))
    al = [inp["a_log_fwd"][0], inp["a_log_bwd"][0]]
    db = [inp["dt_bias_fwd"][0], inp["dt_bias_bwd"][0]]
    if flip:
        al, db = al[::-1], db[::-1]
    m["alog"] = np.ascontiguousarray(np.concatenate(al).reshape(1, 128))
    m["dtb"] = np.ascontiguousarray(np.concatenate(db).reshape(1, 128))
    return m


def prep_shared(inp):
    m = dict(_consts())
    m["w_in"] = np.ascontiguousarray(inp["w_in"][0])
    wq = inp["w_uq"][0].reshape(512, 16, 192)
    m["w_uqn"] = np.ascontiguousarray(wq[:, :, :128].reshape(512, 2048))
    m["w_uqr"] = np.ascontiguousarray(wq[:, :, 128:].reshape(512, 1024))
    m["w_uqrot"] = np.ascontiguousarray(np.concatenate([wq[:, :, 160:192], wq[:, :, 128:160]], 2).reshape(512, 1024))
    wkv = inp["w_ukv"][0].reshape(512, 16, 256)
    m["w_k"] = np.ascontiguousarray(wkv[:, :, :128].reshape(512, 2048))
    m["w_v"] = np.ascontiguousarray(wkv[:, :, 128:].reshape(512, 2048))
    m["w_oattn"] = np.ascontiguousarray(inp["w_o_attn"][0])
    m["w_ossd"] = np.ascontiguousarray(inp["w_o_ssd"][0])
    m["w_out"] = np.ascontiguousarray(inp["w_out"][0])
    m["w_up"] = np.ascontiguousarray(inp["ffn_w_up"][0])
    m["w_down"] = np.ascontiguousarray(inp["ffn_w_down"][0])
    m["nw_mix"] = inp["norm_mix_w"][0].reshape(1, D)
    m["nw_ffn"] = inp["norm_ffn_w"][0].reshape(1, D)
    m["nw_fin"] = inp["norm_final_w"].reshape(1, D)
    m["nw_ssd"] = inp["ssd_norm_w"][0].reshape(1, DI)
    m["nw_q"] = np.ascontiguousarray(inp["q_norm_w"][0].reshape(4, 128).T)
    m["nw_kv"] = np.ascontiguousarray(inp["kv_norm_w"][0].reshape(4, 128).T)
    m["cb"] = np.ascontiguousarray(inp["ssd_conv_b"][0].reshape(48, 128).T)
    m["cb_row"] = inp["ssd_conv_b"][0].reshape(1, 6144)
    m["fb"] = np.ascontiguousarray(inp["ffn_conv_b"][0].reshape(88, 128).T)
    m["dsk"] = inp["ssd_d"][0].reshape(1, 64)
    return {k_: np.ascontiguousarray(v, dtype=np.float32) for k_, v in m.items()}


_CACHE = {}


def run(inputs, cores=8, dbg=False, upto=99, trace=False):
    inputs = {k_: np.asarray(v) for k_, v in inputs.items()}
    bld = Builder(dbg=dbg, upto=upto)
    nc = bld.build()
    shared = prep_shared(inputs)
    in_maps = [prep_core(inputs, c, shared) for c in range(cores)]
    res = run_bass_kernel_spmd(nc, in_maps, core_ids=list(range(cores)), trace=trace)
    return res


def kernel(**inputs):
    res = run(inputs)
    out = np.zeros((4, S, D), np.float32)
    for c in range(8):
        b, flip = c // 2, c % 2
        y = res.results[c]["y"]
        if flip:
            out[b, NOUT:] = y[::-1]
        else:
            out[b, :NOUT] = y
    return out
```

```python
import numpy as np
from contextlib import ExitStack
import concourse.bass as bass
import concourse.mybir as mybir
from concourse.bass_utils import run_bass_kernel_spmd

F32, BF16, I32 = mybir.dt.float32, mybir.dt.bfloat16, mybir.dt.int32
AF = mybir.ActivationFunctionType
ALU = mybir.AluOpType

D = 2048
KC = 16
S = 4096
NOWN = 2176
NOUT = 2048
NOTH = S - NOWN
NCH_OWN = 17
EPS = 1e-6
HEADS = 16
SSD_H = 64
DI = 4096
FFN = 5632
SCALE = 192 ** -0.5

OWN_BLOCKS = [(0, 512), (512, 512), (1024, 512), (1536, 512), (2048, 128)]
OTH_BLOCKS = [(0, 512), (512, 512), (1024, 512), (1536, 384)]
ALL_BLOCKS = [(i * 512, 512) for i in range(8)]


class Sem:
    def __init__(self, h):
        self.h = h
        self.val = 0


class Eng:
    def __init__(self, name, e, sem):
        self.name = name
        self.e = e
        self.sem = sem
        self.seen = {}


class T:
    def __init__(self, h, name):
        self.h = h
        self.name = name
        self.w = None
        self.rs = {}
        self.dsem = None

    def __getitem__(self, k):
        return self.h[k]


class K:
    def __init__(self, nc):
        self.nc = nc
        self.all_sems = []
        self.pe = self._eng("pe", nc.tensor)
        self.act = self._eng("act", nc.scalar)
        self.dve = self._eng("dve", nc.vector)
        self.pool = self._eng("pool", nc.gpsimd)
        self.sp = self._eng("sp", nc.sync)
        self.engs = [self.pe, self.act, self.dve, self.pool, self.sp]
        self.free_dsems = []
        self.n_dsem = 0
        self.phase_tiles = []
        self.uid = 0
        self.rr = 0

    def _eng(self, name, e):
        s = Sem(self.nc.alloc_semaphore("es_" + name))
        self.all_sems.append(s)
        return Eng(name, e, s)

    def take_dsem(self):
        if self.free_dsems:
            return self.free_dsems.pop()
        s = Sem(self.nc.alloc_semaphore("ds%d" % self.n_dsem))
        self.n_dsem += 1
        self.all_sems.append(s)
        return s

    def sb(self, st, shape, dt, name=None):
        self.uid += 1
        nm = "%s_%d" % (name or "sb", self.uid)
        h = st.enter_context(self.nc.sbuf_tensor(nm, list(shape), dt))
        t = T(h, nm)
        self.phase_tiles.append(t)
        return t

    def ps(self, st, shape, dt=F32, name=None):
        self.uid += 1
        nm = "%s_%d" % (name or "ps", self.uid)
        h = st.enter_context(self.nc.psum_tensor(nm, list(shape), dt))
        t = T(h, nm)
        self.phase_tiles.append(t)
        return t

    def _wait(self, E, deps):
        best = {}
        for (s, v) in deps:
            if best.get(s, (None, 0))[1] < v:
                best[s] = (s, v)
        for (s, v) in best.values():
            if E.seen.get(s, 0) < v:
                E.e.wait_ge(s.h, v)
                E.seen[s] = v

    def op(self, E, reads, writes, fn):
        deps = []
        for t in reads:
            if t.w is not None:
                deps.append(t.w)
        for t in writes:
            if t.w is not None and t.w[0] is not E.sem:
                deps.append(t.w)
            for s, v in t.rs.items():
                if s is not E.sem:
                    deps.append((s, v))
        self._wait(E, deps)
        ins = fn()
        E.sem.val += 1
        ins.then_inc(E.sem.h, 1)
        tok = (E.sem, E.sem.val)
        for t in reads:
            t.rs[E.sem] = E.sem.val
        for t in writes:
            t.w = tok
            t.rs = {}
        return ins

    def dma(self, Q, out, in_, reads, writes, st, **kw):
        deps = []
        for t in reads:
            if t.w is not None:
                deps.append(t.w)
        for t in writes:
            if t.w is not None:
                deps.append(t.w)
            for s, v in t.rs.items():
                deps.append((s, v))
        self._wait(Q, deps)
        if st.dsem is None:
            st.dsem = self.take_dsem()
        ins = Q.e.dma_start(out=out, in_=in_, **kw)
        st.dsem.val += 16
        ins.then_inc(st.dsem.h, 16)
        tok = (st.dsem, st.dsem.val)
        for t in reads:
            t.rs[st.dsem] = st.dsem.val
        for t in writes:
            t.w = tok
            t.rs = {}

    def load(self, tile, dst_ap, src_ap, **kw):
        self.dma(self.sp, dst_ap, src_ap, [], [tile], tile, **kw)

    def store(self, tile, dst_ap, src_ap, **kw):
        self.dma(self.sp, dst_ap, src_ap, [tile], [], tile, **kw)

    def barrier(self, skip=()):
        for E in self.engs:
            for s in self.all_sems:
                if s in skip:
                    continue
                if s.val > 0 and E.seen.get(s, 0) < s.val:
                    E.e.wait_ge(s.h, s.val)
                    E.seen[s] = s.val
        for t in self.phase_tiles:
            if t.dsem is not None:
                self.free_dsems.append(t.dsem)
                t.dsem = None
        self.phase_tiles = []

    def evac_eng(self):
        self.rr += 1
        return self.act if (self.rr & 1) else self.dve

    def mm(self, ps_t, out_ap, lhsT_t, lhsT_ap, rhs_t, rhs_ap, start, stop):
        nc = self.nc
        return self.op(self.pe, [lhsT_t, rhs_t], [ps_t],
                       lambda: nc.tensor.matmul(out_ap, lhsT=lhsT_ap, rhs=rhs_ap, start=start, stop=stop))

    def copy(self, E, out_t, out_ap, in_t, in_ap):
        nc = self.nc
        if E is self.act:
            return self.op(E, [in_t], [out_t], lambda: nc.scalar.activation(out=out_ap, in_=in_ap, func=AF.Copy))
        elif E is self.dve:
            return self.op(E, [in_t], [out_t], lambda: nc.vector.tensor_copy(out=out_ap, in_=in_ap))
        else:
            return self.op(E, [in_t], [out_t], lambda: nc.gpsimd.tensor_copy(out=out_ap, in_=in_ap))

    def actf(self, out_t, out_ap, in_t, in_ap, func, extra_reads=(), **kw):
        nc = self.nc
        return self.op(self.act, [in_t] + list(extra_reads), [out_t],
                       lambda: nc.scalar.activation(out=out_ap, in_=in_ap, func=func, **kw))


def cdiv(a, b):
    return (a + b - 1) // b


class Builder:
    def __init__(self, dbg=False, upto=99):
        self.dbg = dbg
        self.upto = upto
        self.nc = bass.Bass("TRN2", target_bir_lowering=False)
        self.k = K(self.nc)
        self.inp = {}
        self.scr = {}
        self.spread_casts = False

    def din(self, name, shape, dt=F32):
        self.inp[name] = self.nc.dram_tensor(name, list(shape), dt, kind="ExternalInput").ap()
        return self.inp[name]

    def dscr(self, name, shape, dt, out=False):
        kind = "ExternalOutput" if (out or (self.dbg and name in self.dbg)) else "Internal"
        self.scr[name] = self.nc.dram_tensor(name, list(shape), dt, kind=kind).ap()
        return self.scr[name]

    def declare(self):
        din, ds = self.din, self.dscr
        din("xl", [S, D]); din("pos", [1, S], I32)
        din("w_in", [D, 15552]); din("w_dt", [D, 128])
        din("w_uqn", [512, 2048]); din("w_uqr", [512, 1024]); din("w_uqrot", [512, 1024])
        din("w_k", [512, 2048]); din("w_v", [512, 2048])
        din("w_oattn", [2048, 2048]); din("w_ossd", [4096, 2048]); din("w_out", [2048, 2048])
        din("w_up", [2048, 2 * FFN]); din("w_down", [FFN, 2048])
        din("nw_mix", [1, D]); din("nw_ffn", [1, D]); din("nw_fin", [1, D]); din("nw_ssd", [1, DI])
        din("nw_q", [128, 4]); din("nw_kv", [128, 4])
        din("cw", [128, 48 * 5]); din("cb", [128, 48]); din("cb_row", [1, 6144])
        din("fw", [128, 88 * 3]); din("fb", [128, 88])
        din("alog", [1, 128]); din("dtb", [1, 128]); din("dsk", [1, 64])
        din("c_ident", [128, 128]); din("c_tri", [128, 256]); din("c_negm", [128, 256])
        din("c_invf", [128, 1]); din("c_sgn", [128, 1])
        for nm, shp in [("w_ossd", [4096, 2048]), ("w_down", [8 * 128 * (FFN // 128), 256])]:
            ds("b_" + nm, shp, BF16)
        ds("nT", [D, S], BF16)
        ds("qlatT", [512, NOWN], F32)
        ds("kvT", [576, S], F32)
        ds("xbcT", [6144, S], BF16)
        ds("gT", [4096, NOWN], BF16)
        ds("sz", [NOWN, DI], BF16)
        ds("dtraw", [S, 128], F32)
        ds("qnT", [HEADS * 128, NOWN], BF16)
        ds("qrT", [HEADS * 64, NOWN], BF16)
        ds("knT", [HEADS * 128, S], BF16)
        ds("krT", [64, S], BF16)
        ds("Vs", [HEADS * 128, 32 * 128], BF16)
        ds("oT", [2048, NOWN], BF16)
        ds("gaoT", [2048, NOWN], F32)
        ds("xs_tok", [S, DI], BF16)
        ds("B_tok", [S, 1024], BF16)
        ds("BCT", [2048, NOWN], BF16)
        ds("yf", [NOWN, DI], F32)
        ds("yn", [NOWN, DI], BF16)
        ds("mixT", [2048, NOWN], BF16)
        ds("h1", [NOWN, D], F32)
        ds("n2T", [D, NOWN], BF16)
        ds("aT", [FFN, NOUT], BF16)
        self.eps_t = T(self.nc.alloc_sbuf_tensor("eps_c", [128, 1], F32), "eps_c")
        self.k.op(self.k.dve, [], [self.eps_t], lambda: self.nc.vector.memset(self.eps_t[:], EPS))
        self.yout = self.nc.dram_tensor("y", [NOUT, D], F32, kind="ExternalOutput").ap()

    def phase_cast(self):
        k, nc = self.k, self.nc
        self.cast_t = T(None, "cast")
        self.cast_t.dsem = k.take_dsem()
        self.pending_casts = []
        for nm in ["w_ossd"]:
            src = self.inp[nm]
            dst = self.scr["b_" + nm]
            R, C = src.shape
            per_row = cdiv(C * 4, 8192)
            rows = max(1, min(R, 1024 // per_row))
            for r0 in range(0, R, rows):
                r1 = min(R, r0 + rows)
                self.pending_casts.append((dst[r0:r1, :], src[r0:r1, :]))
        src = self.inp["w_down"]
        dstv = self.scr["b_w_down"].rearrange("(ob p kc) j -> kc p ob j", ob=8, p=128, kc=FFN // 128)
        for kc in range(FFN // 128):
            self.pending_casts.append((dstv[kc], src[kc * 128:(kc + 1) * 128, :].rearrange("p (ob j) -> p ob j", ob=8)))

    def issue_cast(self, n=1):
        nc = self.nc
        for _ in range(n):
            if not self.pending_casts:
                return
            dst, src = self.pending_casts.pop(0)
            ins = nc.gpsimd.dma_start(out=dst, in_=src, max_dma_last_dim=8192)
            self.cast_t.dsem.val += 16
            ins.then_inc(self.cast_t.dsem.h, 16)

    def rstd_from_ss(self, rstd_t, ss_t, n, np_=128):
        k, nc = self.k, self.nc
        k.op(k.act, [ss_t], [rstd_t], lambda: nc.scalar.activation(
            out=rstd_t[:], in_=ss_t[:], func=AF.Sqrt, scale=1.0 / n, bias=self.eps_t[:np_, 0:1]))
        k.op(k.dve, [rstd_t], [rstd_t], lambda: nc.vector.reciprocal(out=rstd_t[:], in_=rstd_t[:]))

    def norm_transpose(self, st, src_rows_fn, ntiles, nw_name, dstT, tok_off=0, extra=None):
        raise NotImplementedError

    def phase_A(self):
        k, nc = self.k, self.nc
        with ExitStack() as st:
            ident = k.sb(st, [128, 128], F32, "ident")
            identb = k.sb(st, [128, 128], BF16, "identb")
            wb = k.sb(st, [128, D], F32, "nw")
            xin = [k.sb(st, [128, D], F32, "xin") for _ in range(2)]
            xn = [k.sb(st, [128, D], BF16, "xn") for _ in range(2)]
            junk = k.sb(st, [128, D], BF16, "junk")
            ss = [k.sb(st, [128, 1], F32, "ss") for _ in range(2)]
            rstd = [k.sb(st, [128, 1], F32, "rstd") for _ in range(2)]
            stage = [k.sb(st, [128, KC, 512], BF16, "stage") for _ in range(2)]
            pt = [k.ps(st, [128, 8, 128], BF16, "pt") for _ in range(4)]
            k.load(ident, ident[:], self.inp["c_ident"])
            k.load(wb, wb[:], self.inp["nw_mix"].partition_broadcast(128))
            k.copy(k.dve, identb, identb[:], ident, ident[:])
            nT = self.scr["nT"].rearrange("(kc p) t -> p kc t", p=128)
            for tt in range(32):
                g, i = tt // 4, tt % 4
                xi, xo = xin[tt % 2], xn[tt % 2]
                k.load(xi, xi[:], self.inp["xl"][tt * 128:(tt + 1) * 128, :])
                s_, r_ = ss[tt % 2], rstd[tt % 2]
                k.op(k.act, [xi], [junk, s_], lambda: nc.scalar.activation(
                    out=junk[:], in_=xi[:], func=AF.Square, accum_out=s_[:]))
                self.rstd_from_ss(r_, s_, D)
                k.op(k.dve, [xi, r_, wb], [xo], lambda: nc.vector.scalar_tensor_tensor(
                    out=xo[:], in0=xi[:], scalar=r_[:, 0:1], in1=wb[:], op0=ALU.mult, op1=ALU.mult))
                for half in range(2):
                    p_ = pt[(2 * tt + half) % 4]
                    for j in range(8):
                        kc = half * 8 + j
                        k.op(k.pe, [xo, identb], [p_], lambda: nc.tensor.transpose(
                            out=p_[:, j, :], in_=xo[:, kc * 128:(kc + 1) * 128], identity=identb[:]))
                    sg = stage[g % 2]
                    k.copy(k.evac_eng(), sg, sg[:, half * 8:(half + 1) * 8, i * 128:(i + 1) * 128], p_, p_[:])
                if i == 3:
                    sg = stage[g % 2]
                    k.store(sg, nT[:, :, g * 512:(g + 1) * 512], sg[:])
            k.barrier(skip=(self.cast_t.dsem,))

    def wload(self, w_, dst, src, cast):
        k = self.k
        if cast:
            k.dma(k.pool, dst, src, [], [w_], w_)
            if self.spread_casts:
                self.issue_cast(1)
        else:
            k.load(w_, dst, src)

    def proj_fm(self, st, wsrc, F, xT, kc_n, blocks, evac, wt, psums, wcols=512, cast=False):
        k, nc = self.k, self.nc
        wv = wsrc.rearrange("(kc p) f -> p kc f", p=128)
        cnt = 0
        nW = cdiv(F, wcols)

        def ld(wi):
            cols = min(wcols, F - wi * wcols)
            w_ = wt[wi % len(wt)]
            self.wload(w_, w_[:, :, :cols], wv[:, :, wi * wcols:wi * wcols + cols], cast)
        ld(0)
        for wi in range(nW):
            if wi + 1 < nW:
                ld(wi + 1)
            cols = min(wcols, F - wi * wcols)
            w_ = wt[wi % len(wt)]
            for fc in range(cdiv(cols, 128)):
                fw = min(128, cols - fc * 128)
                for bi, (t0, tw) in enumerate(blocks):
                    p_ = psums[cnt % len(psums)]
                    cnt += 1
                    for kc in range(kc_n):
                        k.mm(p_, p_[:fw, :tw], w_, w_[:, kc, fc * 128:fc * 128 + fw], xT, xT[:, kc, t0:t0 + tw],
                             kc == 0, kc == kc_n - 1)
                    evac(p_, wi * wcols + fc * 128, fw, t0, tw, bi == len(blocks) - 1)

    def proj_tm(self, st, wsrc, F, xT, kc_n, tiles, evac, wt, psums, wcols=512, cast=False):
        k, nc = self.k, self.nc
        wv = wsrc.rearrange("(kc p) f -> p kc f", p=128)
        cnt = 0
        nW = cdiv(F, wcols)

        def ld(wi):
            cols = min(wcols, F - wi * wcols)
            w_ = wt[wi % len(wt)]
            self.wload(w_, w_[:, :, :cols], wv[:, :, wi * wcols:wi * wcols + cols], cast)
        ld(0)
        for wi in range(nW):
            if wi + 1 < nW:
                ld(wi + 1)
            cols = min(wcols, F - wi * wcols)
            w_ = wt[wi % len(wt)]
            for tt in tiles:
                p_ = psums[cnt % len(psums)]
                cnt += 1
                for kc in range(kc_n):
                    k.mm(p_, p_[:, :cols], xT, xT[:, kc, tt * 128:(tt + 1) * 128], w_, w_[:, kc, :cols],
                         kc == 0, kc == kc_n - 1)
                evac(p_, tt, wi * wcols, cols)

    def row_evac(self, st, dst, ntok, dt, func=None, tok_off=0, nbuf=3, engs=None):
        k, nc = self.k, self.nc
        stg = [k.sb(st, [128, ntok], dt, "rowstg") for _ in range(nbuf)]
        state = {"i": 0}

        def evac(p_, f0, fw, t0, tw, last):
            s_ = stg[state["i"] % nbuf]
            if func is None:
                k.copy(k.evac_eng(), s_, s_[:fw, t0:t0 + tw], p_, p_[:fw, :tw])
            else:
                k.actf(s_, s_[:fw, t0:t0 + tw], p_, p_[:fw, :tw], func)
            if last:
                k.store(s_, dst[f0:f0 + fw, tok_off:tok_off + ntok], s_[:fw, :])
                state["i"] += 1
        return evac

    def phase_B(self, own):
        k, nc = self.k, self.nc
        self.spread_casts = True
        ntok = NOWN if own else NOTH
        tok0 = 0 if own else NOWN
        blocks = OWN_BLOCKS if own else OTH_BLOCKS
        with ExitStack() as st:
            xT = k.sb(st, [128, KC, ntok], BF16, "xT")
            wt = [k.sb(st, [128, KC, 512], BF16, "wt") for _ in range(2)]
            psums = [k.ps(st, [128, 512], F32, "pB") for _ in range(6)]
            nT = self.scr["nT"].rearrange("(kc p) t -> p kc t", p=128)
            for kc in range(KC):
                k.load(xT, xT[:, kc, :], nT[:, kc, tok0:tok0 + ntok])
            bw = self.inp["w_in"]
            ev32 = self.row_evac(st, self.scr["kvT"], ntok, F32, tok_off=tok0, nbuf=2)
            self.proj_fm(st, bw[:, 512:1088], 576, xT, KC, blocks, ev32, wt, psums, cast=True)
            evb = self.row_evac(st, self.scr["xbcT"], ntok, BF16, tok_off=tok0, nbuf=3)
            self.proj_fm(st, bw[:, 5184:11328], 6144, xT, KC, blocks, evb, wt, psums, cast=True)
            if own:
                evq = self.row_evac(st, self.scr["qlatT"], ntok, F32, nbuf=2)
                self.proj_fm(st, bw[:, 0:512], 512, xT, KC, blocks, evq, wt, psums, cast=True)
                evg = self.row_evac(st, self.scr["gT"], ntok, BF16, func=AF.Sigmoid, nbuf=3)
                self.proj_fm(st, bw[:, 11456:15552], 4096, xT, KC, blocks, evg, wt, psums, cast=True)
            stg_dt = [k.sb(st, [128, 128], F32, "stgdt") for _ in range(3)]
            cnt = {"i": 0}
            dtraw = self.scr["dtraw"]

            def ev_dt(p_, tt, c0, cols):
                s_ = stg_dt[cnt["i"] % 3]
                cnt["i"] += 1
                k.copy(k.evac_eng(), s_, s_[:, :], p_, p_[:, :128])
                k.store(s_, dtraw[tok0 + tt * 128:tok0 + (tt + 1) * 128, :], s_[:, :])
            self.proj_tm(st, self.inp["w_dt"], 128, xT, KC, list(range(ntok // 128)), ev_dt, wt, psums, cast=True)
            if own:
                stg_z = [k.sb(st, [128, 512], BF16, "stgz") for _ in range(3)]
                sz = self.scr["sz"]

                def ev_z(p_, tt, c0, cols):
                    s_ = stg_z[cnt["i"] % 3]
                    cnt["i"] += 1
                    k.actf(s_, s_[:, :cols], p_, p_[:, :cols], AF.Silu)
                    k.store(s_, sz[tt * 128:(tt + 1) * 128, c0:c0 + cols], s_[:, :cols])
                self.proj_tm(st, bw[:, 1088:5184], 4096, xT, KC, list(range(NCH_OWN)), ev_z, wt, psums, cast=True)
            self.spread_casts = False
            if not own:
                self.issue_cast(10 ** 6)
            k.barrier()

    def rope_tables(self, st):
        import math
        k, nc = self.k, self.nc
        pi_ = k.sb(st, [128, S], I32, "posi")
        ang = k.sb(st, [128, S], F32, "ang")
        tmp = k.sb(st, [128, S], F32, "rtmp")
        cos2 = k.sb(st, [128, S], F32, "cos2")
        sin2 = k.sb(st, [128, S], F32, "sin2")
        invf = k.sb(st, [128, 1], F32, "invf")
        sgn = k.sb(st, [128, 1], F32, "sgn")
        k.load(pi_, pi_[:], self.inp["pos"].partition_broadcast(128))
        k.load(invf, invf[:], self.inp["c_invf"])
        k.load(sgn, sgn[:], self.inp["c_sgn"])
        k.op(k.dve, [pi_], [ang], lambda: nc.vector.tensor_copy(out=ang[:], in_=pi_[:]))
        k.op(k.dve, [ang, invf], [ang], lambda: nc.vector.tensor_scalar(
            out=ang[:], in0=ang[:], scalar1=invf[:, 0:1], scalar2=None, op0=ALU.mult))
        ki = pi_
        for which, dst in ((0, sin2), (1, cos2)):
            sh = 0.0 if which == 0 else 0.5 * math.pi
            k.op(k.dve, [ang], [tmp], lambda: nc.vector.tensor_scalar(
                out=tmp[:], in0=ang[:], scalar1=sh, scalar2=1.0 / (2 * math.pi), op0=ALU.add, op1=ALU.mult))
            k.op(k.dve, [tmp], [ki], lambda: nc.vector.tensor_copy(out=ki[:], in_=tmp[:]))
            k.op(k.dve, [ki], [tmp], lambda: nc.vector.tensor_copy(out=tmp[:], in_=ki[:]))
            k.op(k.dve, [tmp, ang], [tmp], lambda: nc.vector.scalar_tensor_tensor(
                out=tmp[:], in0=tmp[:], scalar=-2 * math.pi, in1=ang[:], op0=ALU.mult, op1=ALU.add))
            k.op(k.dve, [tmp], [tmp], lambda: nc.vector.tensor_scalar(
                out=tmp[:], in0=tmp[:], scalar1=sh, scalar2=None, op0=ALU.add))
            k.op(k.dve, [tmp], [tmp], lambda: nc.vector.tensor_scalar(
                out=tmp[:], in0=tmp[:], scalar1=-3.1415925, scalar2=3.1415925, op0=ALU.max, op1=ALU.min))
            k.actf(dst, dst[:], tmp, tmp[:], AF.Sin)
        k.op(k.dve, [sin2, sgn], [sin2], lambda: nc.vector.tensor_scalar(
            out=sin2[:], in0=sin2[:], scalar1=sgn[:, 0:1], scalar2=None, op0=ALU.mult))
        return cos2, sin2, ang, tmp

    def lat_norm_block(self, xsrc_t, xsrc_ap_fn, nw, onesf, sq, pss, rb, out_t, out_ap_fn, tw):
        k, nc = self.k, self.nc
        for kc in range(4):
            k.actf(sq, sq[:, kc, :tw], xsrc_t, xsrc_ap_fn(kc), AF.Square)
        for kc in range(4):
            k.mm(pss, pss[:, :tw], onesf, onesf[:], sq, sq[:, kc, :tw], kc == 0, kc == 3)
        k.op(k.act, [pss], [rb], lambda: nc.scalar.activation(
            out=rb[:, :tw], in_=pss[:, :tw], func=AF.Sqrt, scale=1.0 / 512, bias=self.eps_t[:, 0:1]))
        k.op(k.dve, [rb], [rb], lambda: nc.vector.reciprocal(out=rb[:, :tw], in_=rb[:, :tw]))
        for kc in range(4):
            k.op(k.dve, [xsrc_t, nw, rb], [out_t], lambda: nc.vector.scalar_tensor_tensor(
                out=out_ap_fn(kc), in0=xsrc_ap_fn(kc), scalar=nw[:, kc:kc + 1], in1=rb[:, :tw],
                op0=ALU.mult, op1=ALU.mult))

    def proj_fm_res(self, w_, F, xT, kc_n, blocks, evac, psums, f_start=0):
        k = self.k
        cnt = 0
        for fc in range(cdiv(F, 128)):
            fw = min(128, F - fc * 128)
            for bi, (t0, tw) in enumerate(blocks):
                p_ = psums[cnt % len(psums)]
                cnt += 1
                for kc in range(kc_n):
                    k.mm(p_, p_[:fw, :tw], w_, w_[:, kc, f_start + fc * 128:f_start + fc * 128 + fw],
                         xT, xT[:, kc, t0:t0 + tw], kc == 0, kc == kc_n - 1)
                evac(p_, fc * 128, fw, t0, tw, bi == len(blocks) - 1)

    def phase_C1(self):
        k, nc = self.k, self.nc
        with ExitStack() as st:
            cos2, sin2, ka, kb = self.rope_tables(st)
            onesf = k.sb(st, [128, 128], F32, "onesf")
            k.op(k.dve, [], [onesf], lambda: nc.vector.memset(onesf[:], 1.0))
            nwq = k.sb(st, [128, 4], F32, "nwq")
            k.load(nwq, nwq[:], self.inp["nw_q"])
            qlat = k.sb(st, [128, 4, NOWN], F32, "qlat")
            k.load(qlat, qlat[:], self.scr["qlatT"].rearrange("(kc p) t -> p kc t", p=128))
            qn = k.sb(st, [128, 4, NOWN], BF16, "qn")
            sq = k.sb(st, [128, 4, 512], F32, "sq")
            rb = k.sb(st, [128, 512], F32, "rb")
            psums = [k.ps(st, [128, 512], F32, "pC") for _ in range(6)]
            pss = k.ps(st, [128, 512], F32, "pss")
            wn = k.sb(st, [128, 4, 2048], BF16, "wn")
            wr = k.sb(st, [128, 4, 1024], BF16, "wr")
            wrot = k.sb(st, [128, 4, 1024], BF16, "wrot")
            self.wload(wn, wn[:], self.inp["w_uqn"].rearrange("(kc p) f -> p kc f", p=128), True)
            self.wload(wr, wr[:], self.inp["w_uqr"].rearrange("(kc p) f -> p kc f", p=128), True)
            self.wload(wrot, wrot[:], self.inp["w_uqrot"].rearrange("(kc p) f -> p kc f", p=128), True)
            for (t0, tw) in OWN_BLOCKS:
                self.lat_norm_block(qlat, lambda kc: qlat[:, kc, t0:t0 + tw], nwq, onesf, sq, pss, rb,
                                    qn, lambda kc: qn[:, kc, t0:t0 + tw], tw)
            ev = self.row_evac(st, self.scr["qnT"], NOWN, BF16, nbuf=2)
            self.proj_fm_res(wn, 2048, qn, 4, OWN_BLOCKS, ev, psums)
            t1 = [k.sb(st, [128, 512], F32, "t1") for _ in range(2)]
            t2 = [k.sb(st, [128, 512], F32, "t2") for _ in range(2)]
            rstg = [k.sb(st, [128, NOWN], BF16, "rstg") for _ in range(2)]
            cnt = 0
            for hp in range(8):
                sg = rstg[hp % 2]
                for bi, (t0, tw) in enumerate(OWN_BLOCKS):
                    pa = psums[cnt % 6]; pb = psums[(cnt + 1) % 6]; cnt += 2
                    for kc in range(4):
                        k.mm(pa, pa[:, :tw], wr, wr[:, kc, hp * 128:(hp + 1) * 128], qn, qn[:, kc, t0:t0 + tw], kc == 0, kc == 3)
                    for kc in range(4):
                        k.mm(pb, pb[:, :tw], wrot, wrot[:, kc, hp * 128:(hp + 1) * 128], qn, qn[:, kc, t0:t0 + tw], kc == 0, kc == 3)
                    a_, b_ = t1[bi % 2], t2[bi % 2]
                    k.op(k.dve, [pa, cos2], [a_], lambda: nc.vector.tensor_tensor(
                        out=a_[:, :tw], in0=pa[:, :tw], in1=cos2[:, t0:t0 + tw], op=ALU.mult))
                    k.op(k.dve, [pb, sin2], [b_], lambda: nc.vector.tensor_tensor(
                        out=b_[:, :tw], in0=pb[:, :tw], in1=sin2[:, t0:t0 + tw], op=ALU.mult))
                    k.op(k.pool, [a_, b_], [sg], lambda: nc.gpsimd.tensor_tensor(
                        out=sg[:, t0:t0 + tw], in0=a_[:, :tw], in1=b_[:, :tw], op=ALU.add))
                k.store(sg, self.scr["qrT"][hp * 128:(hp + 1) * 128, :], sg[:])
            kro = k.sb(st, [64, S], BF16, "kro")
            kvT = self.scr["kvT"]
            k.load(ka, ka[0:64, :], kvT[512:576, :])
            k.load(kb, kb[0:32, :], kvT[544:576, :])
            k.load(kb, kb[32:64, :], kvT[512:544, :])
            k.op(k.dve, [ka, cos2], [ka], lambda: nc.vector.tensor_tensor(out=ka[0:64, :], in0=ka[0:64, :], in1=cos2[0:64, :], op=ALU.mult))
            k.op(k.pool, [kb, sin2], [kb], lambda: nc.gpsimd.tensor_tensor(out=kb[0:64, :], in0=kb[0:64, :], in1=sin2[0:64, :], op=ALU.mult))
            k.op(k.dve, [ka, kb], [kro], lambda: nc.vector.tensor_tensor(out=kro[:, :], in0=ka[0:64, :], in1=kb[0:64, :], op=ALU.add))
            k.store(kro, self.scr["krT"][:, :], kro[:, :])
            k.barrier()

    def phase_C2(self):
        k, nc = self.k, self.nc
        with ExitStack() as st:
            onesf = k.sb(st, [128, 128], F32, "onesf")
            k.op(k.dve, [], [onesf], lambda: nc.vector.memset(onesf[:], 1.0))
            nwk = k.sb(st, [128, 4], F32, "nwk")
            k.load(nwk, nwk[:], self.inp["nw_kv"])
            kvn = k.sb(st, [128, 4, S], BF16, "kvn")
            xblk = [k.sb(st, [128, 4, 512], F32, "kvblk") for _ in range(2)]
            sq = k.sb(st, [128, 4, 512], F32, "sq")
            rb = k.sb(st, [128, 512], F32, "rb")
            psums = [k.ps(st, [128, 512], F32, "pC") for _ in range(6)]
            pss = k.ps(st, [128, 512], F32, "pss")
            wk = k.sb(st, [128, 4, 2048], BF16, "wk")
            wv = k.sb(st, [128, 4, 2048], BF16, "wv")
            self.wload(wk, wk[:], self.inp["w_k"].rearrange("(kc p) f -> p kc f", p=128), True)
            self.wload(wv, wv[:], self.inp["w_v"].rearrange("(kc p) f -> p kc f", p=128), True)
            kvT = self.scr["kvT"][0:512, :].rearrange("(kc p) t -> p kc t", p=128)
            for bi, (t0, tw) in enumerate(ALL_BLOCKS):
                xb = xblk[bi % 2]
                k.load(xb, xb[:], kvT[:, :, t0:t0 + tw])
                self.lat_norm_block(xb, lambda kc: xb[:, kc, :tw], nwk, onesf, sq, pss, rb,
                                    kvn, lambda kc: kvn[:, kc, t0:t0 + tw], tw)
            ev = self.row_evac(st, self.scr["knT"], S, BF16, nbuf=2)
            self.proj_fm_res(wk, 2048, kvn, 4, ALL_BLOCKS, ev, psums)
            vst = k.sb(st, [128, 4, 32, 128], BF16, "vst")
            Vs = self.scr["Vs"]
            cnt = 0
            for hg in range(4):
                for tt in range(32):
                    p_ = psums[cnt % 6]; cnt += 1
                    for kc in range(4):
                        k.mm(p_, p_[:, :], kvn, kvn[:, kc, tt * 128:(tt + 1) * 128], wv, wv[:, kc, hg * 512:(hg + 1) * 512], kc == 0, kc == 3)
                    k.copy(k.evac_eng(), vst, vst[:, :, tt, :], p_, p_[:, :].rearrange("p (h d) -> p h d", h=4))
                for hh in range(4):
                    h = hg * 4 + hh
                    k.store(vst, Vs[h * 128:(h + 1) * 128, :], vst[:, hh, :, :].rearrange("p c d -> p (c d)"))
            k.barrier()

    def phase_D(self):
        k, nc = self.k, self.nc
        with ExitStack() as st:
            onesf = k.sb(st, [128, 128], F32, "onesf")
            k.op(k.dve, [], [onesf], lambda: nc.vector.memset(onesf[:], 1.0))
            acc = [k.sb(st, [128, 512], F32, "lacc") for _ in range(2)]
            kr = k.sb(st, [128, S], BF16, "kr")
            k.op(k.dve, [], [kr], lambda: nc.vector.memset(kr[64:128, :], 0.0))
            k.load(kr, kr[0:64, :], self.scr["krT"][:, :])
            kn = [k.sb(st, [128, S], BF16, "kn") for _ in range(2)]
            vv = [k.sb(st, [128, 32, 128], BF16, "vv") for _ in range(2)]
            qn = [k.sb(st, [128, NOWN], BF16, "qn") for _ in range(2)]
            qr = [k.sb(st, [128, NOWN], BF16, "qr") for _ in range(2)]
            for q_ in qr:
                k.op(k.dve, [], [q_], lambda: nc.vector.memset(q_[64:128, :], 0.0))
            pT = [k.sb(st, [128, 512], BF16, "pT") for _ in range(4)]
            rl = [k.sb(st, [128, 512], F32, "rl") for _ in range(2)]
            ostg = [k.sb(st, [128, NOWN], BF16, "ostg") for _ in range(2)]
            ps_s = [k.ps(st, [128, 512], F32, "pS") for _ in range(3)]
            ps_o = [k.ps(st, [128, 512], F32, "pO") for _ in range(2)]
            ps_l = [k.ps(st, [128, 512], F32, "pL") for _ in range(2)]
            NKC = 32

            def hloads(h):
                kn_, vv_, qn_, qr_ = kn[h % 2], vv[h % 2], qn[h % 2], qr[h % 2]
                k.load(kn_, kn_[:, :], self.scr["knT"][h * 128:(h + 1) * 128, :])
                k.load(vv_, vv_[:].rearrange("p c d -> p (c d)"), self.scr["Vs"][h * 128:(h + 1) * 128, :])
                k.load(qn_, qn_[:, :], self.scr["qnT"][h * 128:(h + 1) * 128, :])
                k.load(qr_, qr_[0:64, :], self.scr["qrT"][h * 64:(h + 1) * 64, :])
            items = []
            for h in range(HEADS):
                for bi, (t0, tw) in enumerate(OWN_BLOCKS):
                    items.append((h, bi, t0, tw))
            stream = [(n, kc) for n in range(len(items)) for kc in range(NKC)]

            def qk(n, kc):
                h, bi, t0, tw = items[n]
                kn_, qn_, qr_ = kn[h % 2], qn[h % 2], qr[h % 2]
                p_ = ps_s[(n * NKC + kc) % 3]
                k.mm(p_, p_[:, :tw], kn_, kn_[:, kc * 128:(kc + 1) * 128], qn_, qn_[:, t0:t0 + tw], True, False)
                k.mm(p_, p_[:, :tw], kr, kr[:, kc * 128:(kc + 1) * 128], qr_, qr_[:, t0:t0 + tw], False, True)

            def pv(n, kc):
                h, bi, t0, tw = items[n]
                vv_ = vv[h % 2]
                po = ps_o[n % 2]
                p_ = ps_s[(n * NKC + kc) % 3]
                e_ = pT[kc % 4]
                k.actf(e_, e_[:, :tw], p_, p_[:, :tw], AF.Exp, scale=SCALE)
                k.mm(po, po[:, :tw], vv_, vv_[:, kc, :], e_, e_[:, :tw], kc == 0, kc == NKC - 1)
                ac = acc[kc % 2]
                E_, eng = (k.dve, nc.vector) if kc % 2 == 0 else (k.pool, nc.gpsimd)
                if kc < 2:
                    k.op(E_, [e_], [ac], lambda: eng.tensor_copy(out=ac[:, :tw], in_=e_[:, :tw]))
                else:
                    k.op(E_, [ac, e_], [ac], lambda: eng.tensor_tensor(
                        out=ac[:, :tw], in0=ac[:, :tw], in1=e_[:, :tw], op=ALU.add))

            def fin(n):
                h, bi, t0, tw = items[n]
                po, pl = ps_o[n % 2], ps_l[n % 2]
                og = ostg[h % 2]
                k.mm(pl, pl[:, :tw], onesf, onesf[:], acc[0], acc[0][:, :tw], True, False)
                k.mm(pl, pl[:, :tw], onesf, onesf[:], acc[1], acc[1][:, :tw], False, True)
                r_ = rl[n % 2]
                k.op(k.dve, [pl], [r_], lambda: nc.vector.reciprocal(out=r_[:, :tw], in_=pl[:, :tw]))
                k.op(k.dve, [po, r_], [og], lambda: nc.vector.tensor_tensor(
                    out=og[:, t0:t0 + tw], in0=po[:, :tw], in1=r_[:, :tw], op=ALU.mult))
                if bi == len(OWN_BLOCKS) - 1:
                    k.store(og, self.scr["oT"][h * 128:(h + 1) * 128, :], og[:, :])

            hloads(0)
            hloads(1)
            qk(*stream[0])
            qk(*stream[1])
            for j, (n, kc) in enumerate(stream):
                h, bi, t0, tw = items[n]
                if kc == 0 and bi == 0 and h >= 1 and h + 1 < HEADS:
                    hloads(h + 1)
                if j + 2 < len(stream):
                    qk(*stream[j + 2])
                pv(n, kc)
                if kc == NKC - 1:
                    fin(n)
            k.barrier()

    def phase_E(self):
        k, nc = self.k, self.nc
        with ExitStack() as st:
            xT = k.sb(st, [128, KC, NOWN], BF16, "oTres")
            oT = self.scr["oT"].rearrange("(kc p) t -> p kc t", p=128)
            for kc in range(KC):
                k.load(xT, xT[:, kc, :], oT[:, kc, :])
            wt = [k.sb(st, [128, KC, 512], BF16, "wt") for _ in range(2)]
            psums = [k.ps(st, [128, 512], F32, "pE") for _ in range(6)]
            grow = [k.sb(st, [128, NOWN], BF16, "grow") for _ in range(2)]
            stg = [k.sb(st, [128, NOWN], F32, "estg") for _ in range(2)]
            state = {"i": 0}
            gT = self.scr["gT"]
            gao = self.scr["gaoT"]

            def evac(p_, f0, fw, t0, tw, last):
                g_ = grow[state["i"] % 2]
                s_ = stg[state["i"] % 2]
                if t0 == 0:
                    k.load(g_, g_[:, :], gT[f0:f0 + 128, :])
                k.op(k.dve, [p_, g_], [s_], lambda: nc.vector.tensor_tensor(
                    out=s_[:, t0:t0 + tw], in0=p_[:, :tw], in1=g_[:, t0:t0 + tw], op=ALU.mult))
                if last:
                    k.store(s_, gao[f0:f0 + 128, :], s_[:, :])
                    state["i"] += 1
            self.proj_fm(st, self.inp["w_oattn"], 2048, xT, KC, OWN_BLOCKS, evac, wt, psums, cast=True)
            k.barrier()

    def phase_F(self):
        k, nc = self.k, self.nc
        with ExitStack() as st:
            ident = k.sb(st, [128, 128], F32, "ident")
            identb = k.sb(st, [128, 128], BF16, "identb")
            k.load(ident, ident[:], self.inp["c_ident"])
            k.copy(k.dve, identb, identb[:], ident, ident[:])
            cw = k.sb(st, [128, 240], F32, "cw")
            cb = k.sb(st, [128, 48], F32, "cb")
            cbrow = k.sb(st, [128, 5120], F32, "cbrow")
            k.load(cw, cw[:], self.inp["cw"])
            k.load(cb, cb[:], self.inp["cb"])
            k.load(cbrow, cbrow[:], self.inp["cb_row"][:, 0:5120].partition_broadcast(128))
            xc = [k.sb(st, [128, 4, S + 4], BF16, "xc") for _ in range(2)]
            dg = [k.sb(st, [128, 4, 5, 128], BF16, "dg") for _ in range(2)]
            tmp = [k.sb(st, [128, 512], F32, "ctmp") for _ in range(2)]
            stg = [k.sb(st, [128, 512], BF16, "cstg") for _ in range(3)]
            psums = [k.ps(st, [128, 512], F32, "pF") for _ in range(6)]
            xbcT = self.scr["xbcT"]
            for x_ in xc:
                k.op(k.dve, [], [x_], lambda: nc.vector.memset(x_[:, :, 0:2], 0.0))
                k.op(k.dve, [], [x_], lambda: nc.vector.memset(x_[:, :, S + 2:S + 4], 0.0))
            cnt = 0
            def ldcg(cg):
                x_, d_ = xc[cg % 2], dg[cg % 2]
                for q in range(4):
                    cc = cg * 4 + q
                    k.load(x_, x_[:, q, 2:S + 2], xbcT[cc * 128:(cc + 1) * 128, :])
                    for j in range(5):
                        k.op(k.dve, [identb, cw], [d_], lambda: nc.vector.tensor_scalar(
                            out=d_[:, q, j, :], in0=identb[:], scalar1=cw[:, cc * 5 + j:cc * 5 + j + 1], scalar2=None, op0=ALU.mult))
            ldcg(0)
            for cg in range(10):
                x_, d_ = xc[cg % 2], dg[cg % 2]
                if cg + 1 < 10:
                    ldcg(cg + 1)
                for tt in range(32):
                    p_ = psums[cnt % 6]
                    for q in range(4):
                        for j in range(5):
                            k.mm(p_, p_[:, q * 128:(q + 1) * 128], x_, x_[:, q, tt * 128 + j:tt * 128 + j + 128],
                                 d_, d_[:, q, j, :], j == 0, j == 4)
                    t_ = tmp[cnt % 2]
                    s_ = stg[cnt % 3]
                    cnt += 1
                    k.op(k.dve, [p_, cbrow], [t_], lambda: nc.vector.tensor_tensor(
                        out=t_[:], in0=p_[:], in1=cbrow[:, cg * 512:(cg + 1) * 512], op=ALU.add))
                    k.actf(s_, s_[:], t_, t_[:], AF.Silu)
                    if cg < 8:
                        k.store(s_, self.scr["xs_tok"][tt * 128:(tt + 1) * 128, cg * 512:(cg + 1) * 512], s_[:])
                    else:
                        k.store(s_, self.scr["B_tok"][tt * 128:(tt + 1) * 128, (cg - 8) * 512:(cg - 7) * 512], s_[:])
            xf = [k.sb(st, [128, NOWN + 4], BF16, "xf") for _ in range(2)]
            dgf = [k.sb(st, [128, 5, 128], BF16, "dgf") for _ in range(2)]
            rst = [k.sb(st, [128, NOWN], BF16, "rst") for _ in range(2)]
            for x_ in xf:
                k.op(k.dve, [], [x_], lambda: nc.vector.memset(x_[:, 0:2], 0.0))
            for i, cc in enumerate(range(32, 48)):
                x_, d_, r_ = xf[i % 2], dgf[i % 2], rst[i % 2]
                k.load(x_, x_[:, 2:NOWN + 4], xbcT[cc * 128:(cc + 1) * 128, 0:NOWN + 2])
                for j in range(5):
                    k.op(k.dve, [identb, cw], [d_], lambda: nc.vector.tensor_scalar(
                        out=d_[:, j, :], in0=identb[:], scalar1=cw[:, cc * 5 + j:cc * 5 + j + 1], scalar2=None, op0=ALU.mult))
                for (t0, tw) in OWN_BLOCKS:
                    p_ = psums[cnt % 6]
                    cnt += 1
                    for j in range(5):
                        k.mm(p_, p_[:, :tw], d_, d_[:, j, :], x_, x_[:, t0 + j:t0 + j + tw], j == 0, j == 4)
                    k.op(k.act, [p_, cb], [r_], lambda: nc.scalar.activation(
                        out=r_[:, t0:t0 + tw], in_=p_[:, :tw], func=AF.Silu, bias=cb[:, cc:cc + 1]))
                k.store(r_, self.scr["BCT"][(cc - 32) * 128:(cc - 31) * 128, :], r_[:, :])
            k.barrier()

    def phase_G(self, d):
        k, nc = self.k, self.nc
        with ExitStack() as st:
            nchk = NCH_OWN if d == 0 else 32
            tri = k.sb(st, [128, 128], F32, "tri")
            negm = k.sb(st, [128, 128], F32, "negm")
            negmb = k.sb(st, [128, 128], BF16, "negmb")
            ident = k.sb(st, [128, 128], F32, "ident")
            identb = k.sb(st, [128, 128], BF16, "identb")
            onesf = k.sb(st, [128, 128], F32, "onesf")
            one_c = k.sb(st, [128, 1], F32, "one_c")
            sel = k.sb(st, [128, 64, 128], BF16, "sel")
            k.load(tri, tri[:], self.inp["c_tri"][:, d * 128:(d + 1) * 128])
            k.load(negm, negm[:], self.inp["c_negm"][:, d * 128:(d + 1) * 128])
            k.load(ident, ident[:], self.inp["c_ident"])
            k.copy(k.dve, negmb, negmb[:], negm, negm[:])
            k.copy(k.dve, identb, identb[:], ident, ident[:])
            k.op(k.dve, [], [onesf], lambda: nc.vector.memset(onesf[:], 1.0))
            k.op(k.dve, [], [one_c], lambda: nc.vector.memset(one_c[:], 1.0))
            k.op(k.dve, [ident], [sel], lambda: nc.vector.tensor_copy(
                out=sel[0:64], in_=ident[0:64, 0:64].unsqueeze(2).to_broadcast([64, 64, 128])))
            k.op(k.dve, [], [sel], lambda: nc.vector.memset(sel[64:128], 0.0))
            dt_t = k.sb(st, [128, nchk, 64], F32, "dt_t")
            a_t = k.sb(st, [128, nchk, 64], F32, "a_t")
            dtb_b = k.sb(st, [128, 64], F32, "dtb_b")
            negA = k.sb(st, [128, 64], F32, "negA")
            k.load(dt_t, dt_t[:], self.scr["dtraw"][0:nchk * 128, d * 64:(d + 1) * 64].rearrange("(c p) h -> p c h", p=128))
            k.load(dtb_b, dtb_b[:], self.inp["dtb"][:, d * 64:(d + 1) * 64].partition_broadcast(128))
            k.load(negA, negA[:], self.inp["alog"][:, d * 64:(d + 1) * 64].partition_broadcast(128))
            k.actf(negA, negA[:], negA, negA[:], AF.Exp)
            k.op(k.dve, [negA], [negA], lambda: nc.vector.tensor_scalar(
                out=negA[:], in0=negA[:], scalar1=-1.0, scalar2=None, op0=ALU.mult))
            k.op(k.dve, [dt_t, dtb_b], [dt_t], lambda: nc.vector.tensor_tensor(
                out=dt_t[:], in0=dt_t[:], in1=dtb_b[:].unsqueeze(1).to_broadcast([128, nchk, 64]), op=ALU.add))
            k.actf(dt_t, dt_t[:], dt_t, dt_t[:], AF.Exp)
            k.op(k.act, [dt_t, one_c], [dt_t], lambda: nc.scalar.activation(
                out=dt_t[:], in_=dt_t[:], func=AF.Ln, bias=one_c[:, 0:1]))
            k.op(k.dve, [dt_t, negA], [a_t], lambda: nc.vector.tensor_tensor(
                out=a_t[:], in0=dt_t[:], in1=negA[:].unsqueeze(1).to_broadcast([128, nchk, 64]), op=ALU.mult))
            lndt = k.sb(st, [128, NCH_OWN, 64], F32, "lndt")
            k.actf(lndt, lndt[:], dt_t, dt_t[:, 0:NCH_OWN, :], AF.Ln)
            H = k.sb(st, [128, DI], F32, "H")
            k.op(k.dve, [], [H], lambda: nc.vector.memset(H[:], 0.0))
            prevb = k.sb(st, [128, DI], BF16, "prevb")
            xs_c = [k.sb(st, [128, DI], BF16, "xs_c") for _ in range(2)]
            B_c = [k.sb(st, [128, 1024], BF16, "B_c") for _ in range(2)]
            bct = [k.sb(st, [128, 16, 128], BF16, "bct") for _ in range(2)]
            xd_l = [None, None]
            xdd = k.sb(st, [128, DI], BF16, "xdd")
            dbl = lambda shape, dt, nm: [k.sb(st, shape, dt, nm) for _ in range(2)]
            cs_sb_l = dbl([128, 64], F32, "cs_sb")
            ncs_l = dbl([128, 64], F32, "ncs")
            E_sb_l = dbl([128, 64], F32, "E_sb")
            dte_l = dbl([128, 64], F32, "dte")
            cd_sb_l = dbl([128, 64], F32, "cd_sb")
            w1_l = dbl([128, 64], F32, "w1")
            cshl_l = dbl([128, 2, 128], BF16, "cshl")
            for c_ in cshl_l:
                k.op(k.dve, [], [c_], lambda: nc.vector.memset(c_[64:128], 0.0))
            nbhl_l = dbl([128, 2, 128], BF16, "nbhl")
            for c_ in nbhl_l:
                k.op(k.dve, [], [c_], lambda: nc.vector.memset(c_[64:128], 0.0))
            dec8 = [k.sb(st, [128, 8, 128], F32, "dec8") for _ in range(2)]
            mt8 = [k.sb(st, [128, 8, 128], BF16, "mt8") for _ in range(2)]
            ych_l = [k.sb(st, [128, DI], F32, "ych") for _ in range(2 if d == 0 else 1)]
            psm = k.ps(st, [128, 512], F32, "psm")
            pcbx = st.enter_context(nc.psum_tensor("g_pcbx_%d" % d, [128, 256], F32))
            pcb = [T(pcbx, "pcb0"), T(pcbx, "pcb1")]
            pcb_ap = [pcbx[:, 0:128], pcbx[:, 128:256]]
            k.phase_tiles += pcb
            pseg = [k.ps(st, [128, 512], F32, "pseg") for _ in range(2)]
            pyd = k.ps(st, [128, 512], F32, "pyd")
            pyo = k.ps(st, [128, 512], F32, "pyo")
            pst_l = [k.ps(st, [128, 512], F32, "pst") for _ in range(2)]
            if d == 1:
                yf_t = k.sb(st, [128, DI], F32, "yf_t")
                sz_t = k.sb(st, [128, DI], BF16, "sz_t")
                yn_t = k.sb(st, [128, DI], BF16, "yn_t")
                nwb = k.sb(st, [128, DI], F32, "nwb")
                dsk = k.sb(st, [128, 64], F32, "dsk")
                ss = k.sb(st, [128, 1], F32, "ss")
                rstd = k.sb(st, [128, 1], F32, "rstd")
                k.load(nwb, nwb[:], self.inp["nw_ssd"].partition_broadcast(128))
                k.load(dsk, dsk[:], self.inp["dsk"].partition_broadcast(128))
            other = list(range(31, 16, -1)) if d == 1 else []
            own = list(range(17)) if d == 0 else list(range(16, -1, -1))
            seq = [(c, False) for c in other] + [(c, True) for c in own]
            def prep(it):
                c, is_own = seq[it]
                pb = it % 2
                xd, cs_sb, ncs, E_sb, dte, cd_sb, w1, cshl = (xd_l[pb], cs_sb_l[pb], ncs_l[pb], E_sb_l[pb], dte_l[pb],
                                                              cd_sb_l[pb], w1_l[pb], cshl_l[pb])
                xs_, Bc_ = xs_c[it % 2], B_c[it % 2]
                k.load(xs_, xs_[:], self.scr["xs_tok"][c * 128:(c + 1) * 128, :])
                k.load(Bc_, Bc_[:], self.scr["B_tok"][c * 128:(c + 1) * 128, :])
                a_c = a_t[:, c, :]
                k.mm(psm, psm[:, 0:64], tri, tri[:], a_t, a_c, True, True)
                k.mm(psm, psm[:, 64:128], onesf, onesf[:], a_t, a_c, True, True)
                k.copy(k.act, cs_sb, cs_sb[:], psm, psm[:, 0:64])
                k.op(k.dve, [psm, cs_sb], [dte], lambda: nc.vector.tensor_tensor(
                    out=dte[:], in0=psm[:, 64:128], in1=cs_sb[:], op=ALU.subtract))
                k.actf(dte, dte[:], dte, dte[:], AF.Exp)
                k.actf(cd_sb, cd_sb[:], psm, psm[:, 64:128], AF.Exp)
                k.op(k.dve, [dt_t, dte], [w1], lambda: nc.vector.tensor_tensor(
                    out=w1[:], in0=dt_t[:, c, :], in1=dte[:], op=ALU.mult))
                if is_own:
                    bc_ = bct[it % 2]
                    k.load(bc_, bc_[:], self.scr["BCT"][:, c * 128:(c + 1) * 128].rearrange("(j p) t -> p j t", p=128))
                    k.op(k.dve, [lndt, psm], [ncs], lambda: nc.vector.tensor_tensor(
                        out=ncs[:], in0=lndt[:, c, :], in1=psm[:, 0:64], op=ALU.subtract))
                    k.actf(E_sb, E_sb[:], psm, psm[:, 0:64], AF.Exp)
                    k.op(k.pe, [cs_sb, ident], [psm], lambda: nc.tensor.transpose(
                        out=psm[0:64, 128:256], in_=cs_sb[:], identity=ident[:]))
                    k.copy(k.act, cshl, cshl[0:64, 0, :], psm, psm[0:64, 128:256])
                    k.op(k.dve, [psm, cshl], [cshl], lambda: nc.vector.tensor_tensor(
                        out=cshl[0:64, 1, :], in0=psm[0:64, 128:256], in1=cshl[0:64, 0, :], op=ALU.subtract))
                    nbhl = nbhl_l[pb]
                    k.op(k.pe, [ncs, ident], [psm], lambda: nc.tensor.transpose(
                        out=psm[0:64, 256:384], in_=ncs[:], identity=ident[:]))
                    k.copy(k.act, nbhl, nbhl[0:64, 0, :], psm, psm[0:64, 256:384])
                    k.op(k.dve, [psm, nbhl], [nbhl], lambda: nc.vector.tensor_tensor(
                        out=nbhl[0:64, 1, :], in0=psm[0:64, 256:384], in1=nbhl[0:64, 0, :], op=ALU.subtract))

            def body(it):
                c, is_own = seq[it]
                pb = it % 2
                xd, cs_sb, ncs, E_sb, dte, cd_sb, w1, cshl = (xd_l[pb], cs_sb_l[pb], ncs_l[pb], E_sb_l[pb], dte_l[pb],
                                                              cd_sb_l[pb], w1_l[pb], cshl_l[pb])
                xs_, Bc_ = xs_c[it % 2], B_c[it % 2]
                bc_ = bct[it % 2]
                ych = ych_l[it % len(ych_l)]
                k.op(k.pool, [xs_, w1], [xdd], lambda: nc.gpsimd.tensor_tensor(
                    out=xdd[:].rearrange("p (h q) -> p h q", q=64), in0=xs_[:].rearrange("p (h q) -> p h q", q=64),
                    in1=w1[:].unsqueeze(2).to_broadcast([128, 64, 64]), op=ALU.mult))
                if is_own:
                    if d == 1:
                        k.load(yf_t, yf_t[:], self.scr["yf"][c * 128:(c + 1) * 128, :])
                        k.load(sz_t, sz_t[:], self.scr["sz"][c * 128:(c + 1) * 128, :])
                    k.copy(k.act, prevb, prevb[:], H, H[:])

                    def seg(g):
                        pc_ = pcb[g % 2]
                        k.mm(pc_, pcb_ap[g % 2], bc_, bc_[:, g, :], bc_, bc_[:, 8 + g, :], True, True)
                        d8, m8 = dec8[g % 2], mt8[g % 2]
                        nbhl = nbhl_l[pb]
                        for hh in range(8):
                            h = g * 8 + hh
                            pg_ = pseg[hh // 4]
                            reg = pg_[:, (hh % 4) * 128:(hh % 4 + 1) * 128]
                            k.mm(pg_, reg, sel, sel[:, h, :], cshl, cshl[:, 0, :], True, False)
                            k.mm(pg_, reg, sel, sel[:, h, :], cshl, cshl[:, 1, :], False, False)
                            k.mm(pg_, reg, nbhl, nbhl[:, 0, :], sel, sel[:, h, :], False, False)
                            k.mm(pg_, reg, nbhl, nbhl[:, 1, :], sel, sel[:, h, :], False, False)
                            k.mm(pg_, reg, identb, identb[:], negmb, negmb[:], False, True)
                            if hh % 4 == 3:
                                hb = hh // 4
                                k.op(k.act, [pg_], [d8], lambda: nc.scalar.activation(
                                    out=d8[:, hb * 4:(hb + 1) * 4, :], in_=pg_[:].rearrange("p (a b) -> p a b", a=4),
                                    func=AF.Exp))

                    def rest(g):
                        pc_ = pcb[g % 2]
                        d8, m8 = dec8[g % 2], mt8[g % 2]
                        k.op(k.dve, [d8, pc_], [m8], lambda: nc.vector.tensor_tensor(
                            out=m8[:], in0=d8[:], in1=pcb_ap[g % 2].unsqueeze(1).to_broadcast([128, 8, 128]), op=ALU.mult))
                        for hh in range(8):
                            h = g * 8 + hh
                            k.mm(pyd, pyd[:, hh * 64:(hh + 1) * 64], m8, m8[:, hh, :], xs_, xs_[:, h * 64:(h + 1) * 64], True, True)
                        k.mm(pyo, pyo[:], bc_, bc_[:, 8 + g, :], prevb, prevb[:, g * 512:(g + 1) * 512], True, True)
                        yg = ych[:, g * 512:(g + 1) * 512]
                        k.op(k.dve, [pyo, E_sb], [ych], lambda: nc.vector.tensor_tensor(
                            out=yg.rearrange("p (h q) -> p h q", q=64), in0=pyo[:].rearrange("p (h q) -> p h q", q=64),
                            in1=E_sb[:, g * 8:(g + 1) * 8].unsqueeze(2).to_broadcast([128, 8, 64]), op=ALU.mult))
                        k.op(k.dve, [ych, pyd], [ych], lambda: nc.vector.tensor_tensor(
                            out=yg, in0=yg, in1=pyd[:], op=ALU.add))
                    seg(0)
                    for g in range(8):
                        if g + 1 < 8:
                            seg(g + 1)
                        rest(g)
                k.op(k.pool, [H, cd_sb], [H], lambda: nc.gpsimd.tensor_tensor(
                    out=H[:].rearrange("p (h q) -> p h q", q=64), in0=H[:].rearrange("p (h q) -> p h q", q=64),
                    in1=cd_sb[:].unsqueeze(2).to_broadcast([128, 64, 64]), op=ALU.mult))
                for g in range(8):
                    pst = pst_l[g % 2]
                    k.mm(pst, pst[:], Bc_, Bc_[:, g * 128:(g + 1) * 128], xdd, xdd[:, g * 512:(g + 1) * 512], True, True)
                    Hg = H[:, g * 512:(g + 1) * 512]
                    k.op(k.dve, [H, pst], [H], lambda: nc.vector.tensor_tensor(out=Hg, in0=Hg, in1=pst[:], op=ALU.add))
                if not is_own:
                    return
                if d == 0:
                    k.store(ych, self.scr["yf"][c * 128:(c + 1) * 128, :], ych[:])
                else:
                    k.op(k.pool, [ych, yf_t], [ych], lambda: nc.gpsimd.tensor_tensor(out=ych[:], in0=ych[:], in1=yf_t[:], op=ALU.add))
                    k.op(k.pool, [xs_, dsk], [yf_t], lambda: nc.gpsimd.tensor_tensor(
                        out=yf_t[:].rearrange("p (h q) -> p h q", q=64), in0=xs_[:].rearrange("p (h q) -> p h q", q=64),
                        in1=dsk[:].unsqueeze(2).to_broadcast([128, 64, 64]), op=ALU.mult))
                    k.op(k.dve, [ych, yf_t], [ych], lambda: nc.vector.tensor_tensor(out=ych[:], in0=ych[:], in1=yf_t[:], op=ALU.add))
                    k.op(k.dve, [ych, sz_t], [ych], lambda: nc.vector.tensor_tensor(out=ych[:], in0=ych[:], in1=sz_t[:], op=ALU.mult))
                    k.op(k.act, [ych], [yf_t, ss], lambda: nc.scalar.activation(
                        out=yf_t[:], in_=ych[:], func=AF.Square, accum_out=ss[:]))
                    self.rstd_from_ss(rstd, ss, DI)
                    k.op(k.dve, [ych, rstd, nwb], [yn_t], lambda: nc.vector.scalar_tensor_tensor(
                        out=yn_t[:], in0=ych[:], scalar=rstd[:, 0:1], in1=nwb[:], op0=ALU.mult, op1=ALU.mult))
                    k.store(yn_t, self.scr["yn"][c * 128:(c + 1) * 128, :], yn_t[:])

            prep(0)
            for it in range(len(seq)):
                if it + 1 < len(seq):
                    prep(it + 1)
                body(it)
            k.barrier()

    def phase_H(self):
        k, nc = self.k, self.nc
        with ExitStack() as st:
            ident = k.sb(st, [128, 128], F32, "ident")
            identb = k.sb(st, [128, 128], BF16, "identb")
            k.load(ident, ident[:], self.inp["c_ident"])
            k.copy(k.dve, identb, identb[:], ident, ident[:])
            ynt = [k.sb(st, [128, DI], BF16, "ynt") for _ in range(2)]
            xTb = [k.sb(st, [128, 32, 512], BF16, "xTb") for _ in range(2)]
            wt = [k.sb(st, [128, 32, 512], BF16, "wtH") for _ in range(2)]
            pt = [k.ps(st, [128, 8, 128], BF16, "ptH") for _ in range(2)]
            psums = [k.ps(st, [128, 512], F32, "pH") for _ in range(5)]
            gsr = [k.sb(st, [128, 512], BF16, "gsr") for _ in range(3)]
            gar = [k.sb(st, [128, 512], F32, "gar") for _ in range(3)]
            tmp = [k.sb(st, [128, 512], F32, "tmpH") for _ in range(3)]
            mst = [k.sb(st, [128, 512], BF16, "mst") for _ in range(3)]
            gT, gao, mixT = self.scr["gT"], self.scr["gaoT"], self.scr["mixT"]
            cnt = {"i": 0, "p": 0}

            def tr(bi):
                t0, tw = OWN_BLOCKS[bi]
                xb = xTb[bi % 2]
                for ti in range(tw // 128):
                    tt = t0 // 128 + ti
                    y_ = ynt[tt % 2]
                    k.load(y_, y_[:], self.scr["yn"][tt * 128:(tt + 1) * 128, :])
                    for grp in range(4):
                        p_ = pt[cnt["p"] % 2]
                        cnt["p"] += 1
                        for j in range(8):
                            kc = grp * 8 + j
                            k.op(k.pe, [y_, identb], [p_], lambda: nc.tensor.transpose(
                                out=p_[:, j, :], in_=y_[:, kc * 128:(kc + 1) * 128], identity=identb[:]))
                        k.copy(k.evac_eng(), xb, xb[:, grp * 8:(grp + 1) * 8, ti * 128:(ti + 1) * 128], p_, p_[:])

            def mmb(bi):
                t0, tw = OWN_BLOCKS[bi]
                xb = xTb[bi % 2]

                def evac(p_, f0, fw, t0_, tw_, last, t0=t0, tw=tw):
                    i = cnt["i"] % 3
                    cnt["i"] += 1
                    k.load(gsr[i], gsr[i][:, :tw], gT[2048 + f0:2048 + f0 + 128, t0:t0 + tw])
                    k.load(gar[i], gar[i][:, :tw], gao[f0:f0 + 128, t0:t0 + tw])
                    k.op(k.dve, [p_, gsr[i]], [tmp[i]], lambda: nc.vector.tensor_tensor(
                        out=tmp[i][:, :tw], in0=p_[:, :tw], in1=gsr[i][:, :tw], op=ALU.mult))
                    k.op(k.pool, [tmp[i], gar[i]], [mst[i]], lambda: nc.gpsimd.tensor_tensor(
                        out=mst[i][:, :tw], in0=tmp[i][:, :tw], in1=gar[i][:, :tw], op=ALU.add))
                    k.store(mst[i], mixT[f0:f0 + 128, t0:t0 + tw], mst[i][:, :tw])
                self.proj_fm(st, self.scr["b_w_ossd"], 2048, xb, 32, [(0, tw)], evac, wt, psums)
            tr(0)
            for bi in range(len(OWN_BLOCKS)):
                if bi + 1 < len(OWN_BLOCKS):
                    tr(bi + 1)
                mmb(bi)
            k.barrier()

    def phase_I(self):
        k, nc = self.k, self.nc
        with ExitStack() as st:
            ident = k.sb(st, [128, 128], F32, "ident")
            identb = k.sb(st, [128, 128], BF16, "identb")
            k.load(ident, ident[:], self.inp["c_ident"])
            k.copy(k.dve, identb, identb[:], ident, ident[:])
            wout_l = [k.sb(st, [128, KC, 512], BF16, "wout") for _ in range(4)]
            wo = self.inp["w_out"].rearrange("(kc p) f -> p kc f", p=128)
            for ob in range(4):
                self.wload(wout_l[ob], wout_l[ob][:], wo[:, :, ob * 512:(ob + 1) * 512], True)
            nwb = k.sb(st, [128, D], F32, "nwb")
            k.load(nwb, nwb[:], self.inp["nw_ffn"].partition_broadcast(128))
            mixb = [k.sb(st, [128, KC, 512], BF16, "mixb") for _ in range(2)]
            xin = [k.sb(st, [128, D], F32, "xin") for _ in range(2)]
            h1t = [k.sb(st, [128, D], F32, "h1t") for _ in range(2)]
            xn = [k.sb(st, [128, D], BF16, "xn") for _ in range(2)]
            junk = k.sb(st, [128, D], BF16, "junk")
            ss = [k.sb(st, [128, 1], F32, "ss") for _ in range(2)]
            rstd = [k.sb(st, [128, 1], F32, "rstd") for _ in range(2)]
            stage = [k.sb(st, [128, KC, 512], BF16, "stage") for _ in range(2)]
            psums = [k.ps(st, [128, 512], F32, "pI") for _ in range(4)]
            pt = [k.ps(st, [128, 8, 128], BF16, "ptI") for _ in range(4)]
            mixT = self.scr["mixT"].rearrange("(kc p) t -> p kc t", p=128)
            n2T = self.scr["n2T"].rearrange("(kc p) t -> p kc t", p=128)
            def front(tt):
                m_, xi, h_, xo = mixb[(tt // 4) % 2], xin[tt % 2], h1t[tt % 2], xn[tt % 2]
                if tt % 4 == 0:
                    wblk = min(512, NOWN - tt * 128)
                    k.load(m_, m_[:, :, :wblk], mixT[:, :, tt * 128:tt * 128 + wblk])
                k.load(xi, xi[:], self.inp["xl"][tt * 128:(tt + 1) * 128, :])
                for ob in range(4):
                    p_ = psums[front.cnt % 4]
                    front.cnt += 1
                    for kc in range(KC):
                        k.mm(p_, p_[:], m_, m_[:, kc, (tt % 4) * 128:(tt % 4 + 1) * 128], wout_l[ob], wout_l[ob][:, kc, :], kc == 0, kc == KC - 1)
                    k.op(k.dve, [p_, xi], [h_], lambda: nc.vector.tensor_tensor(
                        out=h_[:, ob * 512:(ob + 1) * 512], in0=p_[:], in1=xi[:, ob * 512:(ob + 1) * 512], op=ALU.add))
                k.store(h_, self.scr["h1"][tt * 128:(tt + 1) * 128, :], h_[:])
                s_, r_ = ss[tt % 2], rstd[tt % 2]
                k.op(k.act, [h_], [junk, s_], lambda: nc.scalar.activation(
                    out=junk[:], in_=h_[:], func=AF.Square, accum_out=s_[:]))
                self.rstd_from_ss(r_, s_, D)
                k.op(k.dve, [h_, r_, nwb], [xo], lambda: nc.vector.scalar_tensor_tensor(
                    out=xo[:], in0=h_[:], scalar=r_[:, 0:1], in1=nwb[:], op0=ALU.mult, op1=ALU.mult))

            def back(tt):
                g, i = tt // 4, tt % 4
                xo = xn[tt % 2]
                sg = stage[g % 2]
                for half in range(2):
                    p_ = pt[(2 * tt + half) % 4]
                    for j in range(8):
                        kc = half * 8 + j
                        k.op(k.pe, [xo, identb], [p_], lambda: nc.tensor.transpose(
                            out=p_[:, j, :], in_=xo[:, kc * 128:(kc + 1) * 128], identity=identb[:]))
                    k.copy(k.evac_eng(), sg, sg[:, half * 8:(half + 1) * 8, i * 128:(i + 1) * 128], p_, p_[:])
                if i == 3 or tt == NCH_OWN - 1:
                    w_ = (i + 1) * 128
                    k.store(sg, n2T[:, :, g * 512:g * 512 + w_], sg[:, :, :w_])
            front.cnt = 0
            front(0)
            for tt in range(NCH_OWN):
                if tt + 1 < NCH_OWN:
                    front(tt + 1)
                back(tt)
            k.barrier()

    def phase_J(self):
        k, nc = self.k, self.nc
        NT = NOUT
        with ExitStack() as st:
            n2 = k.sb(st, [128, KC, NOWN], BF16, "n2res")
            n2T = self.scr["n2T"].rearrange("(kc p) t -> p kc t", p=128)
            for kc in range(KC):
                k.load(n2, n2[:, kc, :], n2T[:, kc, :])
            fw = k.sb(st, [128, 264], F32, "fw")
            fb = k.sb(st, [128, 88], F32, "fb")
            k.load(fw, fw[:], self.inp["fw"])
            k.load(fb, fb[:], self.inp["fb"])
            wg = [k.sb(st, [128, KC, 512], BF16, "wg") for _ in range(2)]
            wv = [k.sb(st, [128, KC, 512], BF16, "wv") for _ in range(2)]
            pre = [k.sb(st, [128, NT + 2], F32, "pre") for _ in range(2)]
            acc = [k.sb(st, [128, NT], F32, "acc") for _ in range(2)]
            sg = k.sb(st, [128, NT], F32, "sg")
            ast = [k.sb(st, [128, NT], BF16, "ast") for _ in range(2)]
            psums = [k.ps(st, [128, 512], F32, "pJ") for _ in range(6)]
            for p_ in pre:
                k.op(k.dve, [], [p_], lambda: nc.vector.memset(p_[:, 0:1], 0.0))
            blocks = [(0, 512), (512, 512), (1024, 512), (1536, 512), (2048, 1)]
            wup = self.inp["w_up"].rearrange("(kc p) f -> p kc f", p=128)
            cnt = 0
            def ldw(wi):
                self.wload(wg[wi % 2], wg[wi % 2][:], wup[:, :, wi * 512:(wi + 1) * 512], True)
                self.wload(wv[wi % 2], wv[wi % 2][:], wup[:, :, FFN + wi * 512:FFN + (wi + 1) * 512], True)
            ldw(0)
            for i in range(44):
                wi, q = i // 4, i % 4
                if q == 0 and wi + 1 < 11:
                    ldw(wi + 1)
                for which in range(2):
                    w_ = (wg if which == 0 else wv)[wi % 2]
                    ch = i if which == 0 else 44 + i
                    pr, ac = pre[which], acc[which]
                    for (t0, tw) in blocks:
                        p_ = psums[cnt % 6]
                        cnt += 1
                        for kc in range(KC):
                            k.mm(p_, p_[:, :tw], w_, w_[:, kc, q * 128:(q + 1) * 128], n2, n2[:, kc, t0:t0 + tw], kc == 0, kc == KC - 1)
                        k.copy(k.act, pr, pr[:, 1 + t0:1 + t0 + tw], p_, p_[:, :tw])
                    k.op(k.dve, [pr, fw, fb], [ac], lambda: nc.vector.tensor_scalar(
                        out=ac[:], in0=pr[:, 0:NT], scalar1=fw[:, ch * 3:ch * 3 + 1], scalar2=fb[:, ch:ch + 1],
                        op0=ALU.mult, op1=ALU.add))
                    for j in (1, 2):
                        k.op(k.dve, [pr, fw, ac], [ac], lambda: nc.vector.scalar_tensor_tensor(
                            out=ac[:], in0=pr[:, j:j + NT], scalar=fw[:, ch * 3 + j:ch * 3 + j + 1], in1=ac[:],
                            op0=ALU.mult, op1=ALU.add))
                k.actf(sg, sg[:], acc[0], acc[0][:], AF.Silu)
                a_ = ast[i % 2]
                k.op(k.pool, [sg, acc[1]], [a_], lambda: nc.gpsimd.tensor_tensor(
                    out=a_[:], in0=sg[:], in1=acc[1][:], op=ALU.mult))
                k.store(a_, self.scr["aT"][i * 128:(i + 1) * 128, :], a_[:])
            k.barrier()

    def phase_K(self):
        k, nc = self.k, self.nc
        NKF = FFN // 128
        with ExitStack() as st:
            aTb = [k.sb(st, [128, NKF, 512], BF16, "aTb") for _ in range(2)]
            wd = [k.sb(st, [128, NKF, 256], BF16, "wd") for _ in range(2)]
            h1t = [k.sb(st, [128, D], F32, "h1k") for _ in range(4)]
            outt = [k.sb(st, [128, D], F32, "outt") for _ in range(2)]
            nwb = k.sb(st, [128, D], F32, "nwb")
            junk = k.sb(st, [128, D], BF16, "junk")
            ss = [k.sb(st, [128, 1], F32, "ss") for _ in range(2)]
            rstd = [k.sb(st, [128, 1], F32, "rstd") for _ in range(2)]
            psums = [k.ps(st, [128, 512], F32, "pK") for _ in range(6)]
            k.load(nwb, nwb[:], self.inp["nw_fin"].partition_broadcast(128))
            aT = self.scr["aT"].rearrange("(kc p) t -> p kc t", p=128)
            wdn = self.scr["b_w_down"].rearrange("(ob p kc) j -> ob p kc j", ob=8, p=128, kc=NKF)
            cnt = 0
            wcnt = 0
            k.load(aTb[0], aTb[0][:], aT[:, :, 0:512])
            for tb in range(NOUT // 512):
                a_ = aTb[tb % 2]
                if tb + 1 < NOUT // 512:
                    k.load(aTb[(tb + 1) % 2], aTb[(tb + 1) % 2][:], aT[:, :, (tb + 1) * 512:(tb + 2) * 512])
                for ti in range(4):
                    tt = tb * 4 + ti
                    k.load(h1t[ti], h1t[ti][:], self.scr["h1"][tt * 128:(tt + 1) * 128, :])
                for ob in range(8):
                    w_ = wd[wcnt % 2]
                    wcnt += 1
                    k.load(w_, w_[:], wdn[ob])
                    for ti in range(4):
                        p_ = psums[cnt % 6]
                        cnt += 1
                        for kc in range(NKF):
                            k.mm(p_, p_[:, :256], a_, a_[:, kc, ti * 128:(ti + 1) * 128], w_, w_[:, kc, :], kc == 0, kc == NKF - 1)
                        h_ = h1t[ti]
                        k.op(k.dve, [p_, h_], [h_], lambda: nc.vector.tensor_tensor(
                            out=h_[:, ob * 256:(ob + 1) * 256], in0=p_[:, :256], in1=h_[:, ob * 256:(ob + 1) * 256], op=ALU.add))
                for ti in range(4):
                    tt = tb * 4 + ti
                    h_, o_, s_, r_ = h1t[ti], outt[ti % 2], ss[ti % 2], rstd[ti % 2]
                    k.op(k.act, [h_], [junk, s_], lambda: nc.scalar.activation(
                        out=junk[:], in_=h_[:], func=AF.Square, accum_out=s_[:]))
                    self.rstd_from_ss(r_, s_, D)
                    k.op(k.dve, [h_, r_, nwb], [o_], lambda: nc.vector.scalar_tensor_tensor(
                        out=o_[:], in0=h_[:], scalar=r_[:, 0:1], in1=nwb[:], op0=ALU.mult, op1=ALU.mult))
                    k.store(o_, self.yout[tt * 128:(tt + 1) * 128, :], o_[:])
            k.barrier()

    def finish(self):
        k, nc = self.k, self.nc
        k.barrier()

    def build(self):
        self.declare()
        phases = [self.phase_cast, self.phase_A, lambda: self.phase_B(True), lambda: self.phase_B(False),
                  self.phase_C1, self.phase_C2, self.phase_D, self.phase_E,
                  self.phase_F, lambda: self.phase_G(0), lambda: self.phase_G(1),
                  self.phase_H, self.phase_I, self.phase_J, self.phase_K]
        for i, p in enumerate(phases):
            if i > self.upto:
                break
            p()
        self.finish()
        return self.nc


def _consts():
    ident = np.eye(128, dtype=np.float32)
    s = np.arange(128)[:, None]
    l = np.arange(128)[None, :]
    tri_f = (s <= l).astype(np.float32)
    tri_b = (s >= l).astype(np.float32)
    neg_f = np.where(l >= s, 0.0, -30000.0).astype(np.float32)
    neg_b = np.where(l <= s, 0.0, -30000.0).astype(np.float32)
    half = 32
    invf = (10000.0 ** (-np.arange(half, dtype=np.float32) / half)).astype(np.float32)
    invf128 = np.tile(invf, 4).reshape(128, 1)
    sgn = np.tile(np.concatenate([-np.ones(32, np.float32), np.ones(32, np.float32)]), 2).reshape(128, 1)
    return dict(c_ident=ident, c_tri=np.concatenate([tri_f, tri_b], 1), c_negm=np.concatenate([neg_f, neg_b], 1),
                c_invf=invf128, c_sgn=sgn)


def prep_core(inp, c, shared):
    b, flip = c // 2, c % 2
    x = inp["x"][b]
    pos = inp["positions"][b]
    if flip:
        x = x[::-1]
        pos = pos[::-1]
    m = dict(shared)
    m["xl"] = np.ascontiguousarray(x, dtype=np.float32)
    m["pos"] = np.ascontiguousarray(pos, dtype=np.int32).reshape(1, S)
    w_in = inp["w_in"][0]
    dtf, dtb_ = w_in[:, 11328:11392], w_in[:, 11392:11456]
    m["w_dt"] = np.ascontiguousarray(np.concatenate([dtb_, dtf] if flip else [dtf, dtb_], 1))
    cw = inp["ssd_conv_w"][0]
    fw = inp["ffn_conv_w"][0]
    if flip:
        cw = cw[::-1]
        fw = fw[::-1]
    m["cw"] = np.ascontiguousarray(cw.reshape(5, 48, 128).transpose(2, 1, 0).reshape(128, 240))
    m["fw"] = np.ascontiguousarray(fw.reshape(3, 88, 128).transpose(2, 1, 0).reshape(128, 264))
    al = [inp["a_log_fwd"][0], inp["a_log_bwd"][0]]
    db = [inp["dt_bias_fwd"][0], inp["dt_bias_bwd"][0]]
    if flip:
        al, db = al[::-1], db[::-1]
    m["alog"] = np.ascontiguousarray(np.concatenate(al).reshape(1, 128))
    m["dtb"] = np.ascontiguousarray(np.concatenate(db).reshape(1, 128))
    return m


def prep_shared(inp):
    m = dict(_consts())
    m["w_in"] = np.ascontiguousarray(inp["w_in"][0])
    wq = inp["w_uq"][0].reshape(512, 16, 192)
    m["w_uqn"] = np.ascontiguousarray(wq[:, :, :128].reshape(512, 2048))
    m["w_uqr"] = np.ascontiguousarray(wq[:, :, 128:].reshape(512, 1024))
    m["w_uqrot"] = np.ascontiguousarray(np.concatenate([wq[:, :, 160:192], wq[:, :, 128:160]], 2).reshape(512, 1024))
    wkv = inp["w_ukv"][0].reshape(512, 16, 256)
    m["w_k"] = np.ascontiguousarray(wkv[:, :, :128].reshape(512, 2048))
    m["w_v"] = np.ascontiguousarray(wkv[:, :, 128:].reshape(512, 2048))
    m["w_oattn"] = np.ascontiguousarray(inp["w_o_attn"][0])
    m["w_ossd"] = np.ascontiguousarray(inp["w_o_ssd"][0])
    m["w_out"] = np.ascontiguousarray(inp["w_out"][0])
    m["w_up"] = np.ascontiguousarray(inp["ffn_w_up"][0])
    m["w_down"] = np.ascontiguousarray(inp["ffn_w_down"][0])
    m["nw_mix"] = inp["norm_mix_w"][0].reshape(1, D)
    m["nw_ffn"] = inp["norm_ffn_w"][0].reshape(1, D)
    m["nw_fin"] = inp["norm_final_w"].reshape(1, D)
    m["nw_ssd"] = inp["ssd_norm_w"][0].reshape(1, DI)
    m["nw_q"] = np.ascontiguousarray(inp["q_norm_w"][0].reshape(4, 128).T)
    m["nw_kv"] = np.ascontiguousarray(inp["kv_norm_w"][0].reshape(4, 128).T)
    m["cb"] = np.ascontiguousarray(inp["ssd_conv_b"][0].reshape(48, 128).T)
    m["cb_row"] = inp["ssd_conv_b"][0].reshape(1, 6144)
    m["fb"] = np.ascontiguousarray(inp["ffn_conv_b"][0].reshape(88, 128).T)
    m["dsk"] = inp["ssd_d"][0].reshape(1, 64)
    return {k_: np.ascontiguousarray(v, dtype=np.float32) for k_, v in m.items()}


_CACHE = {}


def run(inputs, cores=8, dbg=False, upto=99, trace=False):
    inputs = {k_: np.asarray(v) for k_, v in inputs.items()}
    bld = Builder(dbg=dbg, upto=upto)
    nc = bld.build()
    shared = prep_shared(inputs)
    in_maps = [prep_core(inputs, c, shared) for c in range(cores)]
    res = run_bass_kernel_spmd(nc, in_maps, core_ids=list(range(cores)), trace=trace)
    return res


def kernel(**inputs):
    res = run(inputs)
    out = np.zeros((4, S, D), np.float32)
    for c in range(8):
        b, flip = c // 2, c % 2
        y = res.results[c]["y"]
        if flip:
            out[b, NOUT:] = y[::-1]
        else:
            out[b, :NOUT] = y
    return out
```
